# Optimizing a Trainium2 kernel written in Bass

```python
import jax, jax.numpy as jnp
from jax import lax
import numpy as np

D_MODEL = 1024
BATCH = 16
SEQ = 4096
DEPTH = 2
DEC_BATCH = 16
DEC_SEQ = 32
PAST_LEN = 2048

CHUNK = 64
N_AB_LAYERS = (DEPTH + 1) // 2
N_C_LAYERS = DEPTH // 2
A_WIDTH = D_MODEL // 2
A_HEAD = 128
A_HEADS = A_WIDTH // A_HEAD
B_WIDTH = D_MODEL // 2
B_HEADS = 4
B_KEY_WIDTH = B_WIDTH // 2
B_DK = B_KEY_WIDTH // B_HEADS
B_DV = B_WIDTH // B_HEADS
GLA_GATE_RANK = 16
GLA_GATE_NORMALIZER = 16.0
AB_SPLITS = (A_WIDTH, A_WIDTH, A_WIDTH, A_WIDTH, B_KEY_WIDTH, B_KEY_WIDTH, B_WIDTH, GLA_GATE_RANK, B_WIDTH)
AB_IN_WIDTH = 4 * A_WIDTH + 2 * B_KEY_WIDTH + 2 * B_WIDTH + GLA_GATE_RANK
AB_OUT_WIDTH = A_WIDTH + B_WIDTH
C_HEAD = 64
C_HEADS = D_MODEL // C_HEAD
C_DECAY_RANK = 64
C_AAA_RANK = 64
C_GATE_RANK = 128
C_GN_EPS = 64e-5
PEER_HEADS = 8
PEER_N_KEYS = 128
PEER_N_EXPERTS = PEER_N_KEYS * PEER_N_KEYS
PEER_TOPK = 16
PEER_QUERY = 256
PEER_SUBKEY = PEER_QUERY // 2
PEER_BLOCK = 256
NORM_EPS = 1e-6

kernel_name = 'hgrn2_gla_rwkv7_peer_stream_step'


def rms_norm(x, g):
    xf = x.astype(jnp.float32)
    y = xf * lax.rsqrt(jnp.mean(xf * xf, axis=-1, keepdims=True) + NORM_EPS)
    return (y * g.astype(jnp.float32)).astype(x.dtype)


def chunked_gated_recurrence(q, k, v, log_a, s0):
    bsz, T, H, _ = q.shape
    V = v.shape[-1]
    c = min(CHUNK, T)
    n = -(-T // c)
    pad = n * c - T

    def blocks(t):
        t = jnp.pad(t, ((0, 0), (0, pad), (0, 0), (0, 0)))
        return jnp.moveaxis(t.reshape(bsz, n, c, H, t.shape[-1]), 1, 0)

    qs, ks, vs, gs = blocks(q), blocks(k), blocks(v), blocks(log_a.astype(jnp.float32))
    causal = jnp.tril(jnp.ones((c, c), dtype=bool))[None, :, :, None, None]

    def step(S, inp):
        qc, kc, vc, gc = inp
        b = jnp.cumsum(gc, axis=1)
        diff = jnp.where(causal, b[:, :, None] - b[:, None, :], -jnp.inf)
        attn = jnp.einsum('bthk,bshk,btshk->bhts', qc, kc, jnp.exp(diff))
        o = (jnp.einsum('bhts,bshv->bthv', attn, vc)
             + jnp.einsum('bthk,bhkv->bthv', qc * jnp.exp(b), S))
        b_last = b[:, -1]
        S = (jnp.exp(b_last)[..., None] * S
             + jnp.einsum('bshk,bshv->bhkv', kc * jnp.exp(b_last[:, None] - b), vc))
        return S, o

    S, o = lax.scan(step, s0.astype(jnp.float32), (qs, ks, vs, gs))
    o = jnp.moveaxis(o, 0, 1).reshape(bsz, n * c, H, V)[:, :T]
    return o, S


def hgrn2_gla_mixer(h, s_hgrn, s_gla, w_in, lb, hgrn_norm_g, gla_gate_w2, gla_gate_b, gla_norm_g, w_out):
    bsz, T, _ = h.shape
    z = h @ w_in
    offs = [int(o) for o in np.cumsum(AB_SPLITS)[:-1]]
    a_q, a_f, a_i, a_gate, b_q, b_k, b_v, b_lr, b_gate = jnp.split(z, offs, axis=-1)

    def heads(t, H):
        return t.reshape(bsz, T, H, -1)

    f = lb + (1.0 - lb) * jax.nn.sigmoid(a_f.astype(jnp.float32))
    o_a, s_a = chunked_gated_recurrence(heads(jax.nn.silu(a_q), A_HEADS), heads(1.0 - f, A_HEADS),
                                        heads(a_i, A_HEADS), heads(jnp.log(f), A_HEADS), s_hgrn)
    o_a = rms_norm(o_a, hgrn_norm_g).reshape(bsz, T, A_WIDTH) * jax.nn.silu(a_gate)

    log_g = jax.nn.log_sigmoid((b_lr @ gla_gate_w2 + gla_gate_b).astype(jnp.float32)) / GLA_GATE_NORMALIZER
    o_b, s_b = chunked_gated_recurrence(heads(b_q * (B_DK ** -0.5), B_HEADS), heads(b_k, B_HEADS),
                                        heads(b_v, B_HEADS), heads(log_g, B_HEADS), s_gla)
    o_b = rms_norm(o_b, gla_norm_g).reshape(bsz, T, B_WIDTH) * jax.nn.silu(b_gate)

    y = jnp.concatenate([o_a, o_b], axis=-1).astype(h.dtype) @ w_out
    return y, s_a, s_b


def rwkv7_recurrence(r, w, k, v, kk, a, s0):
    def tm(t):
        return jnp.moveaxis(t.astype(jnp.float32), 1, 0)

    def step(S, inp):
        r_t, w_t, k_t, v_t, kk_t, a_t = inp
        sa = jnp.einsum('bhij,bhj->bhi', S, kk_t)
        S = (S * w_t[:, :, None, :] - sa[..., None] * (kk_t * a_t)[:, :, None, :]
             + v_t[..., None] * k_t[:, :, None, :])
        return S, jnp.einsum('bhij,bhj->bhi', S, r_t)

    S, o = lax.scan(step, s0.astype(jnp.float32), (tm(r), tm(w), tm(k), tm(v), tm(kk), tm(a)))
    return jnp.moveaxis(o, 0, 1), S


def rwkv7_mixer(h, s_wkv, x_last, mu, w_rkv, w_w1, w_w2, w0, a_w1, a_w2, a0, g_w1, g_w2,
                k_k, k_a, r_k, ln_g, ln_b, w_out):
    bsz, T, D = h.shape
    x_prev = jnp.concatenate([x_last[:, None].astype(h.dtype), h[:, :-1]], axis=1)
    dx = x_prev - h

    def mix(i):
        return h + dx * mu[i]

    r = mix(0) @ w_rkv[0]
    k = mix(1) @ w_rkv[1]
    v = mix(2) @ w_rkv[2]
    w = -jax.nn.softplus(-(w0 + jnp.tanh(mix(3) @ w_w1) @ w_w2).astype(jnp.float32)) - 0.5
    decay = jnp.exp(-jnp.exp(w))
    a = jax.nn.sigmoid((a0 + (mix(4) @ a_w1) @ a_w2).astype(jnp.float32))
    g = jax.nn.sigmoid(mix(5) @ g_w1) @ g_w2

    def hd(t):
        return t.reshape(bsz, T, C_HEADS, C_HEAD)

    kk = hd((k * k_k).astype(jnp.float32))
    kk = kk * lax.rsqrt(jnp.sum(kk * kk, axis=-1, keepdims=True) + 1e-12)
    k = k * (1.0 + (a - 1.0) * k_a)
    rh, kh, vh = hd(r), hd(k), hd(v)
    o, s_new = rwkv7_recurrence(rh, hd(decay), kh, vh, kk, hd(a), s_wkv)
    mean = jnp.mean(o, axis=-1, keepdims=True)
    var = jnp.mean(jnp.square(o - mean), axis=-1, keepdims=True)
    o = ((o - mean) * lax.rsqrt(var + C_GN_EPS)).reshape(bsz, T, D) * ln_g + ln_b
    bonus = jnp.sum(rh * kh * r_k, axis=-1, keepdims=True) * vh
    o = o + bonus.reshape(bsz, T, D)
    y = (o * g).astype(h.dtype) @ w_out
    return y, s_new, h[:, -1]


def peer_ffn(h, w_q, sub_keys, u_tab, v_tab):
    bsz, T, D = h.shape
    n = bsz * T
    blk = min(PEER_BLOCK, n)
    nb = -(-n // blk)
    xt = jnp.pad(h.reshape(n, D), ((0, nb * blk - n), (0, 0))).reshape(nb, blk, D)

    def block(xb):
        q = (xb @ w_q).reshape(blk, PEER_HEADS, 2, PEER_SUBKEY)
        s = jnp.einsum('thpd,hpnd->thpn', q, sub_keys).astype(jnp.float32)
        sv, si = lax.top_k(s, PEER_TOPK)
        cand = (sv[:, :, 0, :, None] + sv[:, :, 1, None, :]).reshape(blk, PEER_HEADS, PEER_TOPK * PEER_TOPK)
        cidx = (si[:, :, 0, :, None] * PEER_N_KEYS + si[:, :, 1, None, :]).reshape(blk, PEER_HEADS, PEER_TOPK * PEER_TOPK)
        cs, ci = lax.top_k(cand, PEER_TOPK)
        eidx = jnp.take_along_axis(cidx, ci, axis=-1)
        gate = jax.nn.softmax(cs, axis=-1)
        hid = jnp.einsum('td,thkd->thk', xb, u_tab[eidx]).astype(jnp.float32)
        act = (jax.nn.gelu(hid, approximate=False) * gate).astype(xb.dtype)
        return jnp.einsum('thk,thkd->td', act, v_tab[eidx])

    out = lax.map(block, xt).reshape(nb * blk, D)[:n]
    return out.reshape(bsz, T, D)


def setup_inputs(seed: int = 0) -> dict:
    key = jax.random.key(seed)
    ks = iter(jax.random.split(key, 48))

    def nrm(shape, scale):
        return jax.random.normal(next(ks), shape, jnp.float32) * scale

    def gain(shape):
        return 1.0 + nrm(shape, 0.02)

    D = D_MODEL
    return {
        'x_prompt': nrm((BATCH, SEQ, D), 1.0),
        'x_sample': nrm((DEC_BATCH, DEC_SEQ, D), 1.0),
        'state_hgrn': nrm((N_AB_LAYERS, DEC_BATCH, A_HEADS, A_HEAD, A_HEAD), 0.1),
        'state_gla': nrm((N_AB_LAYERS, DEC_BATCH, B_HEADS, B_DK, B_DV), 0.1),
        'state_rwkv': nrm((N_C_LAYERS, DEC_BATCH, C_HEADS, C_HEAD, C_HEAD), 0.1),
        'state_shift': nrm((N_C_LAYERS, DEC_BATCH, D), 1.0),
        'w_in_ab': nrm((N_AB_LAYERS, D, AB_IN_WIDTH), D ** -0.5),
        'hgrn_lower_bounds': nrm((DEPTH + 1, A_WIDTH), 0.1),
        'hgrn_norm_g': gain((N_AB_LAYERS, A_HEAD)),
        'gla_gate_w2': nrm((N_AB_LAYERS, GLA_GATE_RANK, B_KEY_WIDTH), GLA_GATE_RANK ** -0.5),
        'gla_gate_b': nrm((N_AB_LAYERS, B_KEY_WIDTH), 0.01),
        'gla_norm_g': gain((N_AB_LAYERS, B_DV)),
        'w_out_ab': nrm((N_AB_LAYERS, AB_OUT_WIDTH, D), AB_OUT_WIDTH ** -0.5),
        'rwkv_mu': jax.random.uniform(next(ks), (N_C_LAYERS, 6, D), jnp.float32),
        'rwkv_w_rkv': nrm((N_C_LAYERS, 3, D, D), D ** -0.5),
        'rwkv_w_w1': nrm((N_C_LAYERS, D, C_DECAY_RANK), D ** -0.5),
        'rwkv_w_w2': nrm((N_C_LAYERS, C_DECAY_RANK, D), 0.2),
        'rwkv_w0': -2.0 + nrm((N_C_LAYERS, D), 0.5),
        'rwkv_a_w1': nrm((N_C_LAYERS, D, C_AAA_RANK), D ** -0.5),
        'rwkv_a_w2': nrm((N_C_LAYERS, C_AAA_RANK, D), 0.2),
        'rwkv_a0': nrm((N_C_LAYERS, D), 0.1),
        'rwkv_g_w1': nrm((N_C_LAYERS, D, C_GATE_RANK), D ** -0.5),
        'rwkv_g_w2': nrm((N_C_LAYERS, C_GATE_RANK, D), C_GATE_RANK ** -0.5),
        'rwkv_k_k': 0.85 + nrm((N_C_LAYERS, D), 0.02),
        'rwkv_k_a': gain((N_C_LAYERS, D)),
        'rwkv_r_k': nrm((N_C_LAYERS, C_HEADS, C_HEAD), 0.1),
        'rwkv_ln_g': gain((N_C_LAYERS, D)),
        'rwkv_ln_b': nrm((N_C_LAYERS, D), 0.01),
        'w_out_c': nrm((N_C_LAYERS, D, D), D ** -0.5),
        'norm1_g': gain((DEPTH, D)),
        'norm2_g': gain((DEPTH, D)),
        'final_g': gain((D,)),
        'peer_w_q': nrm((DEPTH, D, PEER_HEADS * PEER_QUERY), D ** -0.5),
        'peer_sub_keys': nrm((DEPTH, PEER_HEADS, 2, PEER_N_KEYS, PEER_SUBKEY), PEER_SUBKEY ** -0.5),
        'peer_u': nrm((DEPTH, PEER_N_EXPERTS, D), D ** -0.5),
        'peer_v': nrm((DEPTH, PEER_N_EXPERTS, D), 0.1),
    }


def reference(x_prompt, x_sample, state_hgrn, state_gla, state_rwkv, state_shift,
              w_in_ab, hgrn_lower_bounds, hgrn_norm_g, gla_gate_w2, gla_gate_b, gla_norm_g, w_out_ab,
              rwkv_mu, rwkv_w_rkv, rwkv_w_w1, rwkv_w_w2, rwkv_w0, rwkv_a_w1, rwkv_a_w2, rwkv_a0,
              rwkv_g_w1, rwkv_g_w2, rwkv_k_k, rwkv_k_a, rwkv_r_k, rwkv_ln_g, rwkv_ln_b, w_out_c,
              norm1_g, norm2_g, final_g, peer_w_q, peer_sub_keys, peer_u, peer_v):
    lbs = jnp.cumsum(jax.nn.softmax(hgrn_lower_bounds.astype(jnp.float32), axis=0), axis=0)

    def trunk(x, st_h, st_g, st_r, st_s):
        out_h, out_g, out_r, out_s = [], [], [], []
        for l in range(DEPTH):
            j = l // 2
            hn = rms_norm(x, norm1_g[l])
            if l % 2 == 0:
                y, sh, sg = hgrn2_gla_mixer(hn, st_h[j], st_g[j], w_in_ab[j], lbs[l], hgrn_norm_g[j],
                                            gla_gate_w2[j], gla_gate_b[j], gla_norm_g[j], w_out_ab[j])
                out_h.append(sh)
                out_g.append(sg)
            else:
                y, sr, ss = rwkv7_mixer(hn, st_r[j], st_s[j], rwkv_mu[j], rwkv_w_rkv[j], rwkv_w_w1[j],
                                        rwkv_w_w2[j], rwkv_w0[j], rwkv_a_w1[j], rwkv_a_w2[j], rwkv_a0[j],
                                        rwkv_g_w1[j], rwkv_g_w2[j], rwkv_k_k[j], rwkv_k_a[j], rwkv_r_k[j],
                                        rwkv_ln_g[j], rwkv_ln_b[j], w_out_c[j])
                out_r.append(sr)
                out_s.append(ss)
            x = x + y.astype(x.dtype)
            x = x + peer_ffn(rms_norm(x, norm2_g[l]), peer_w_q[l], peer_sub_keys[l],
                             peer_u[l], peer_v[l]).astype(x.dtype)
        return rms_norm(x, final_g), jnp.stack(out_h), jnp.stack(out_g), jnp.stack(out_r), jnp.stack(out_s)

    bp = x_prompt.shape[0]
    z_h = jnp.zeros((N_AB_LAYERS, bp, A_HEADS, A_HEAD, A_HEAD), jnp.float32)
    z_g = jnp.zeros((N_AB_LAYERS, bp, B_HEADS, B_DK, B_DV), jnp.float32)
    z_r = jnp.zeros((N_C_LAYERS, bp, C_HEADS, C_HEAD, C_HEAD), jnp.float32)
    z_s = jnp.zeros((N_C_LAYERS, bp, D_MODEL), x_prompt.dtype)
    y_prompt, p_hgrn, p_gla, p_rwkv, p_shift = trunk(x_prompt, z_h, z_g, z_r, z_s)
    y_sample, s_hgrn, s_gla, s_rwkv, s_shift = trunk(x_sample, state_hgrn, state_gla, state_rwkv, state_shift)
    return (y_prompt, y_sample, p_hgrn, p_gla, p_rwkv, p_shift, s_hgrn, s_gla, s_rwkv, s_shift)
```

```python
import os
import numpy as np
from contextlib import ExitStack
import concourse.bass as bass
import concourse.mybir as mybir
from concourse.bass_utils import run_bass_kernel_spmd

F32 = mybir.dt.float32
BF16 = mybir.dt.bfloat16
U32 = mybir.dt.uint32
I32 = mybir.dt.int32
AF = mybir.ActivationFunctionType
ALU = mybir.AluOpType
AX = mybir.AxisListType

ENGS = ("pe", "act", "dve", "pool", "sp")
NOSAME = set(os.environ.get("K_NOSAME", "pe").split(","))
NCORES = 8
D = 1024
NEG = -1.0e30


class Buf:
    __slots__ = ("t", "name", "lw", "rd")

    def __init__(self, t, name):
        self.t = t
        self.name = name
        self.lw = None
        self.rd = {}

    def __getitem__(self, idx):
        return self.t[idx]


class Prog:
    def __init__(self, nc):
        self.nc = nc
        self.q = {e: [] for e in ENGS}
        self.cnt = {}
        self.known = {e: {} for e in ENGS}
        self.nins = 0
        self.capture = None
        self.pending = []

    def replay_pending(self, n=None):
        k = len(self.pending) if n is None else min(n, len(self.pending))
        for _ in range(k):
            eng, fn, reads, writes, dma = self.pending.pop(0)
            self.op(eng, fn, reads, writes, dma)

    def _deps(self, eng, reads, writes):
        deps = {}

        def add(d):
            if d is None:
                return
            k, v = d
            if deps.get(k, 0) < v:
                deps[k] = v
        for b in reads:
            add(b.lw)
        for b in writes:
            add(b.lw)
            for k, v in b.rd.items():
                add((k, v))
        waits = []
        kn = self.known[eng]
        for k, v in deps.items():
            if k == "c_" + eng and eng in NOSAME:
                continue
            if kn.get(k, 0) < v:
                kn[k] = v
                waits.append((k, v))
        return waits

    def op(self, eng, fn, reads=(), writes=(), dma=None):
        if self.capture is not None:
            self.capture.append((eng, fn, tuple(reads), tuple(writes), dma))
            return
        waits = self._deps(eng, reads, writes)
        if dma:
            key = "d_" + dma
            step = 16
        else:
            key = "c_" + eng
            step = 1
        self.cnt[key] = self.cnt.get(key, 0) + step
        val = self.cnt[key]
        self.q[eng].append((waits, fn, key, step))
        for b in reads:
            if b.rd.get(key, 0) < val:
                b.rd[key] = val
        for b in writes:
            b.lw = (key, val)
            b.rd = {}
        self.nins += 1

    def final_wait(self, eng):
        waits = [(k, v) for k, v in self.cnt.items()]
        self.q[eng].append((waits, None, None, 0))

    def build(self, es):
        nc = self.nc
        sems = {k: es.enter_context(nc.semaphore(k)) for k in sorted(self.cnt.keys())}
        q = self.q

        def replay(name, e):
            for waits, fn, key, step in q[name]:
                for k, v in waits:
                    e.wait_ge(sems[k], v)
                if fn is not None:
                    fn(e).then_inc(sems[key], step)

        with nc.Block() as block:
            @block.tensor
            def _(e):
                replay("pe", e)

            @block.scalar
            def _(e):
                replay("act", e)

            @block.vector
            def _(e):
                replay("dve", e)

            @block.gpsimd
            def _(e):
                replay("pool", e)

            @block.sync
            def _(e):
                replay("sp", e)


PCOL = {}
_pc = 0
for _n, _w in [("n1g0", 8), ("n1g1", 8), ("n2g0", 8), ("n2g1", 8), ("lbraw", 12), ("hng", 1), ("gng", 1),
               ("ggb", 2), ("mu", 48), ("w0", 8), ("a0", 8), ("kk", 8), ("ka", 8), ("rk", 8),
               ("lng", 8), ("lnb", 8)]:
    PCOL[_n] = _pc
    _pc += _w
NPCOL = _pc
NC2 = 1088


def build_program(SEQ, do_l0peer=True, do_l1=True, do_l1peer=True, dbg=99):
    NSP = SEQ // 128
    nc = bass.Bass("TRN2", target_bir_lowering=False)

    def din(name, shape, dt=F32):
        return nc.dram_tensor(name, list(shape), dt, kind="ExternalInput").ap()

    def dout(name, shape, dt=F32):
        return nc.dram_tensor(name, list(shape), dt, kind="ExternalOutput").ap()

    def dscr(name, shape, dt):
        return nc.dram_tensor(name, list(shape), dt, kind="Internal").ap()

    xp = din("xp", [2, SEQ, D])
    xs = din("xs", [2, 32, D])
    st_h = din("st_h", [2, 4, 128, 128])
    st_g = din("st_g", [2, 4, 64, 128])
    st_r = din("st_r", [2, 16, 64, 64])
    st_s = din("st_s", [2, D])
    pcols_d = din("pcols", [128, NPCOL])
    consts_d = din("consts", [128, 1024])
    fing_d = din("fing", [1, D])
    w_in_d = din("w_in", [D, 3600])
    ggw2_d = din("ggw2", [16, 256])
    w_oab_d = din("w_oab", [D, D])
    w_q_d = din("w_q", [2, D, 2048])
    skT_d = din("skT", [2, 128, 16, 128])
    w_rkv_d = din("w_rkv", [3, D, D])
    w_oc_d = din("w_oc", [D, D])
    w_w1_d = din("w_w1", [D, 64])
    a_w1_d = din("a_w1", [D, 64])
    g_w1_d = din("g_w1", [D, 128])
    w_w2_d = din("w_w2", [64, D])
    a_w2_d = din("a_w2", [64, D])
    g_w2_d = din("g_w2", [128, D])
    consts2_d = din("consts2", [128, NC2])
    pu_d = din("pu", [2, 16384, D])
    pv_d = din("pv", [2, 16384, D])
    yp = dout("yp", [2, SEQ, D])
    ys = dout("ys", [2, 32, D])
    o_ph = dout("o_ph", [2, 4, 128, 128])
    o_pg = dout("o_pg", [2, 4, 64, 128])
    o_pr = dout("o_pr", [2, 16, 64, 64])
    o_ps = dout("o_ps", [2, D])
    o_sh = dout("o_sh", [2, 4, 128, 128])
    o_sg = dout("o_sg", [2, 4, 64, 128])
    o_sr = dout("o_sr", [2, 16, 64, 64])
    o_ss = dout("o_ss", [2, D])
    NGRP = 25
    wscr = dscr("wscr", [NGRP, 128, 4096], BF16)
    uvb = dscr("uvb", [32768, 2048], BF16)

    es = ExitStack()
    with es:
        P = Prog(nc)

        def sb(name, shape, dt=F32):
            return Buf(es.enter_context(nc.sbuf_tensor(name, list(shape), dt)), name)

        PS = [Buf(es.enter_context(nc.psum_tensor(f"ps{i}", [128, 512], F32)), f"ps{i}") for i in range(8)]
        psi = [0]

        def bank():
            b = PS[psi[0] % 6]
            psi[0] += 1
            return b

        def MM(out, lhsT, rhs, start, stop, reads, wb):
            P.op("pe", lambda e: e.matmul(out, lhsT, rhs, start=start, stop=stop), reads=reads, writes=[wb])

        def TR(out, in_, ident, reads, wb):
            P.op("pe", lambda e: e.transpose(out, in_, ident), reads=reads, writes=[wb])

        def ACT(out, in_, func, reads, writes, bias=None, scale=None, accum=None):
            kw = {}
            if accum is not None:
                kw["accum_out"] = accum
            if bias is not None:
                kw["bias"] = bias
            if scale is not None:
                kw["scale"] = scale
            P.op("act", lambda e: e.activation(out=out, in_=in_, func=func, **kw), reads=reads, writes=writes)

        def TT(out, in0, in1, op, reads, writes, eng="dve"):
            P.op(eng, lambda e: e.tensor_tensor(out, in0, in1, op), reads=reads, writes=writes)

        def TS(out, in0, s1, s2, op0, op1, reads, writes, eng="dve"):
            if op1 is None:
                P.op(eng, lambda e: e.tensor_scalar(out, in0, s1, None, op0), reads=reads, writes=writes)
            else:
                P.op(eng, lambda e: e.tensor_scalar(out, in0, s1, s2, op0, op1), reads=reads, writes=writes)

        def STT(out, in0, scalar, in1, op0, op1, reads, writes, accum=None):
            if accum is None:
                P.op("dve", lambda e: e.scalar_tensor_tensor(out=out, in0=in0, scalar=scalar, in1=in1, op0=op0, op1=op1),
                     reads=reads, writes=writes)
            else:
                P.op("dve", lambda e: e.scalar_tensor_tensor(out=out, in0=in0, scalar=scalar, in1=in1, op0=op0, op1=op1,
                                                             accum_out=accum), reads=reads, writes=writes)

        def CP(out, in_, reads, writes, eng="dve"):
            if eng == "act":
                ACT(out, in_, AF.Copy, reads, writes)
            else:
                P.op(eng, lambda e: e.tensor_copy(out, in_), reads=reads, writes=writes)

        def RECIP(out, in_, reads, writes):
            P.op("dve", lambda e: e.reciprocal(out, in_), reads=reads, writes=writes)

        def DMA(out, in_, reads, writes, key, eng="sp"):
            P.op(eng, lambda e: e.dma_start(out=out, in_=in_), reads=reads, writes=writes, dma=key)

        def flat(buf):
            t = buf.t
            nd = len(t.shape)
            if nd == 2:
                return buf[:, :]
            if nd == 3:
                return buf[:].rearrange("p a b -> p (a b)")
            return buf[:].rearrange("p a b c -> p (a b c)")

        def SCAN(out, d0, d1, reads, writes):
            P.op("dve", lambda e: e.tensor_tensor_scan(out, d0, d1, 0.0, ALU.mult, ALU.add), reads=reads, writes=writes)

        def RED(out, in_, reads, writes):
            P.op("dve", lambda e: e.tensor_reduce(out=out, in_=in_, axis=AX.X, op=ALU.add), reads=reads, writes=writes)

        pc = sb("pc", [128, NPCOL])
        cst = sb("cst", [128, 1024])
        cstb = sb("cstb", [128, 128], BF16)
        fingb = sb("fingb", [128, D])
        eps_col = sb("eps_col", [128, 4])
        DMA(pc[:], pcols_d, [], [pc], "pc")
        DMA(cst[:], consts_d, [], [cst], "cst")
        DMA(fingb[:], fing_d.partition_broadcast(128), [], [fingb], "fingb")
        CP(cstb[:], cst[:, 0:128], [cst], [cstb])
        P.op("pool", lambda e: e.memset(eps_col[:, 0:1], 1e-6), writes=[eps_col])
        P.op("pool", lambda e: e.memset(eps_col[:, 1:2], 1.0), writes=[eps_col])
        P.op("pool", lambda e: e.memset(eps_col[:, 2:3], 1e-12), writes=[eps_col])
        P.op("pool", lambda e: e.memset(eps_col[:, 3:4], 64e-5), writes=[eps_col])
        ident = cst[:, 0:128]
        identb = cstb[:, 0:128]
        cmask = cst[:, 128:256]
        ones = cst[:, 256:384]
        rst64 = cst[:, 512:640]
        rst32 = cst[:, 672:800]
        iota16 = cst[:, 640:656]
        thr15 = cst[:, 656:671]
        hmask = (cst[:, 384:385], cst[:, 448:449])

        def pcol(name, i=0):
            c = PCOL[name] + i
            return pc[:, c:c + 1]

        dcol = sb("dcol", [128, 16])
        lbe = sb("lbe", [128, 12])
        lbs_ = sb("lbs_", [128, 4])
        ACT(lbe[:], pc[:, PCOL["lbraw"]:PCOL["lbraw"] + 12], AF.Exp, [pc], [lbe])
        TT(lbs_[:], lbe[:, 0:4], lbe[:, 4:8], ALU.add, [lbe], [lbs_])
        TT(lbs_[:], lbs_[:], lbe[:, 8:12], ALU.add, [lbe, lbs_], [lbs_])
        RECIP(lbs_[:], lbs_[:], [lbs_], [lbs_])
        TT(dcol[:, 0:4], lbe[:, 0:4], lbs_[:], ALU.mult, [lbe, lbs_], [dcol])
        TS(dcol[:, 4:8], dcol[:, 0:4], -1.0, 1.0, ALU.mult, ALU.add, [dcol], [dcol])
        TS(dcol[:, 8:10], pc[:, PCOL["ggb"]:PCOL["ggb"] + 2], -1.0, None, ALU.mult, None, [pc], [dcol])

        Fb = [sb(f"F{i}", [128, 512]) for i in range(12)]
        Hb = [sb(f"H{i}", [128, 512], BF16) for i in range(13)]
        xtok = sb("xtok", [128, D])
        acc = sb("acc", [128, D])
        junk = sb("junk", [128, D])
        h2tok = sb("h2tok", [128, D])
        rstd = sb("rstd", [128, 128])
        XTs = [sb("XT", [128, 8, 128]), sb("XT2", [128, 8, 128])]
        xcur = [0]

        def XT_():
            return XTs[xcur[0]]
        hT = sb("hT", [128, 8, 128], BF16)

        scH = [sb("scA", [128, 8, 128]), sb("scB", [128, 8, 128])]
        sc2H = [sb("sc2A", [128, 8, 128]), sb("sc2B", [128, 8, 128])]
        qT_ = sb("qT", [128, 16, 128], BF16)
        stageb = sb("stageb", [128, 2048], BF16)
        scr = Buf(None, "scr")

        def convert_group(g, src2d, col0):
            v = src2d.rearrange("(c p) n -> p c n", p=128)
            for h in range(4):
                DMA(junk[:, :].rearrange("p (c n) -> p c n", n=512), v[:, 2 * h:2 * h + 2, col0:col0 + 512], [], [junk], "junk")
                CP(stageb[:, 0:1024], junk[:, :], [junk], [stageb], eng=("act" if h % 2 == 0 else "dve"))
                DMA(wscr[g, :, h * 1024:(h + 1) * 1024], stageb[:, 0:1024], [stageb], [scr], "st_stageb")

        SL = int(os.environ.get('K_SETUP', '9'))
        G_IN = {"a_q": 0, "a_f": 1, "a_i": 2, "a_gate": 3, "b_qk": 4, "b_v": 5, "b_gate": 6}
        for nm, c0 in (("a_q", 0), ("a_f", 512), ("a_i", 1024), ("a_gate", 1536), ("b_qk", 2048), ("b_v", 2560),
                       ("b_gate", 3088)):
            if SL >= 2:
                convert_group(G_IN[nm], w_in_d, c0)
        G_OAB = 7
        G_WQ = 9
        if SL >= 3:
            convert_group(7, w_oab_d, 0)
            convert_group(8, w_oab_d, 512)
            for l in range(2):
                for q in range(4):
                    convert_group(G_WQ + l * 4 + q, w_q_d[l], q * 512)

        G_RKV = 17
        G_OC = 23
        if SL >= 3:
            for i in range(3):
                for q in range(2):
                    convert_group(G_RKV + i * 2 + q, w_rkv_d[i], q * 512)
            for q in range(2):
                convert_group(G_OC + q, w_oc_d, q * 512)
        W1r = sb("W1r", [128, 8, 256], BF16)
        W2w = sb("W2w", [64, D], BF16)
        W2a = sb("W2a", [64, D], BF16)
        W2g = sb("W2g", [128, D], BF16)
        cst2 = sb("cst2", [128, NC2])
        E2b = sb("E2b", [128, 64], BF16)
        DMA(cst2[:], consts2_d, [], [cst2], "cst2")
        CP(E2b[:], cst2[:, 1024:1088], [cst2], [E2b])
        for (src, c0, nn) in ((w_w1_d, 0, 64), (a_w1_d, 64, 64), (g_w1_d, 128, 128)):
            DMA(junk[:, 0:8 * nn].rearrange("p (c n) -> p c n", n=nn), src.rearrange("(c p) n -> p c n", p=128), [], [junk], "junk")
            CP(W1r[:, :, c0:c0 + nn], junk[:, 0:8 * nn].rearrange("p (c n) -> p c n", n=nn), [junk], [W1r])
        for (src, dst, rows) in ((w_w2_d, W2w, 64), (a_w2_d, W2a, 64), (g_w2_d, W2g, 128)):
            DMA(junk[0:rows, :], src, [], [junk], "junk")
            CP(dst[:], junk[0:rows, :], [junk], [dst])
        dcol2 = sb("dcol2", [128, 8])
        TS(dcol2[:], pc[:, PCOL["ka"]:PCOL["ka"] + 8], -1.0, 1.0, ALU.mult, ALU.add, [pc], [dcol2])
        SK = sb("SK", [128, 2, 16, 128], BF16)
        GW2 = sb("GW2", [128, 256], BF16)
        WLR = sb("WLR", [128, 8, 128], BF16)
        P.op("pool", lambda e: e.memset(GW2[:], 0.0), writes=[GW2])
        P.op("pool", lambda e: e.memset(WLR[:], 0.0), writes=[WLR])
        for l in range(2 if SL >= 4 else 0):
            DMA(junk[:, :], skT_d[l].rearrange("p g n -> p (g n)")[:, 0:1024], [], [junk], "junk")
            CP(SK[:, l, 0:8, :].rearrange("p g n -> p (g n)"), junk[:, :], [junk], [SK])
            DMA(junk[:, :], skT_d[l].rearrange("p g n -> p (g n)")[:, 1024:2048], [], [junk], "junk")
            CP(SK[:, l, 8:16, :].rearrange("p g n -> p (g n)"), junk[:, :], [junk], [SK])
        DMA(junk[0:16, 0:256], ggw2_d, [], [junk], "junk")
        CP(GW2[0:16, :], junk[0:16, 0:256], [junk], [GW2])
        DMA(junk[:, 0:128].rearrange("p (c n) -> p c n", n=16), w_in_d.rearrange("(c p) n -> p c n", p=128)[:, :, 3072:3088],
            [], [junk], "junk")
        CP(WLR[:, :, 0:16], junk[:, 0:128].rearrange("p (c n) -> p c n", n=16), [junk], [WLR])

        NWR = 4
        wring = [sb(f"wr{i}", [128, 8, 512], BF16) for i in range(NWR)]
        wri = [0]

        def wload(g):
            i = wri[0] % NWR
            wri[0] += 1
            wb_ = wring[i]
            DMA(wb_[:].rearrange("p c n -> p (c n)"), wscr[g], [scr], [wb_], f"wr{i}")
            return wb_

        uvsL = [Buf(None, f"uvs{i}") for i in range(4)]
        if SL >= 5:
            step = 0
            fins = (scH[0], scH[1], sc2H[0], sc2H[1])
            fouts = ((stageb, stageb[:, 0:1024]), (stageb, stageb[:, 1024:2048]),
                     (qT_, qT_[:].rearrange("p g n -> p (g n)")[:, 0:1024]), (qT_, qT_[:].rearrange("p g n -> p (g n)")[:, 1024:2048]))
            fobufs = [Buf(stageb.t, "fo0"), Buf(stageb.t, "fo1"), Buf(qT_.t, "fo2"), Buf(qT_.t, "fo3")]
            for fob_, par_ in zip(fobufs, (stageb, stageb, qT_, qT_)):
                fob_.lw = par_.lw
                fob_.rd = dict(par_.rd)
            for l in range(2):
                for (tab, tc) in ((pu_d, 0), (pv_d, 1024)):
                    for b in range(128):
                        fin = fins[step % 4]
                        fo = fouts[step % 4][1]
                        fob = fobufs[step % 4]
                        r0 = b * 128
                        DMA(fin[:].rearrange("p g n -> p (g n)"), tab[l, r0:r0 + 128, :], [], [fin], "cv" + str(step % 4))
                        CP(fo, fin[:].rearrange("p g n -> p (g n)"), [fin], [fob], eng=("act" if step % 2 == 0 else "dve"))
                        DMA(uvb[l * 16384 + r0:l * 16384 + r0 + 128, tc:tc + 1024], fo, [fob], [uvsL[step % 4]], "stcv" + str(step % 4), eng="act")
                        step += 1
            for fob in fobufs:
                par = stageb if fob.t is stageb.t else qT_
                for kk_, vv_ in list(fob.rd.items()) + ([fob.lw] if fob.lw else []):
                    if par.rd.get(kk_, 0) < vv_:
                        par.rd[kk_] = vv_

        Sa = sb("Sa", [128, 4, 128])
        Sg = sb("Sg", [128, 2, 128])
        SabC = [sb(f"Sab{i}", [128, 4, 128], BF16) for i in range(2)]
        SgbC = [sb(f"Sgb{i}", [128, 2, 128], BF16) for i in range(2)]
        blrb = sb("blrb", [128, 128], BF16)
        ooTA = sb("ooTA", [128, 512], BF16)
        ooTB = sb("ooTB", [128, 512], BF16)

        HS = sb("HS", [128, 8, 129])
        Pst = sb("Pst", [128, 8, 64])
        GC = sb("GC", [128, 8, 2])
        NSLOT = 4
        slots = [sb(f"g{i}", [128, D]) for i in range(NSLOT)]
        qT = qT_
        sv = sb("sv", [128, 16, 16])
        siu = sb("siu", [128, 16, 16], U32)
        sif = sb("sif", [128, 16, 16])
        csv = sb("csv", [128, 8, 16])
        ciu = sb("ciu", [128, 8, 16], U32)
        cif = sb("cif", [128, 8, 16])
        t4H = [sb("t4A", [128, 4, 16, 16]), sb("t4B", [128, 4, 16, 16])]
        av = sb("av", [128, 8, 16])
        bv = sb("bv", [128, 8, 16])
        iv = sb("iv", [128, 8, 16])
        jv = sb("jv", [128, 8, 16])
        eidf = sb("eidf", [128, 128])
        eidi = sb("eidi", [128, 128], I32)
        gate = sb("gate", [128, 128])
        gsum = sb("gsum", [128, 8])
        hid = sb("hid", [128, 128])
        actv = sb("actv", [128, 128])
        fssq = sb("fssq", [128, 2])
        DkB = [sb(f"Dk{i}", [128, 128], BF16) for i in range(4)]
        gslot = [0]

        def rms_rstd(W):
            ACT(Fb[0][:, :4 * W].rearrange("p (c w) -> p c w", w=W), XT_()[:, 0:4, :W], AF.Square, [XT_()], [Fb[0]])
            ACT(Fb[1][:, :4 * W].rearrange("p (c w) -> p c w", w=W), XT_()[:, 4:8, :W], AF.Square, [XT_()], [Fb[1]])
            pb = bank()
            for c in range(8):
                src = Fb[c // 4]
                MM(pb[:, :W], ones, src[:, (c % 4) * W:(c % 4 + 1) * W], c == 0, c == 7, [cst, src], pb)
            ACT(rstd[:, :W], pb[:, :W], AF.Sqrt, [pb, eps_col], [rstd], bias=eps_col[:, 0:1], scale=1.0 / D)
            RECIP(rstd[:, :W], rstd[:, :W], [rstd], [rstd])

        def load_x(src_ap, W):
            DMA(xtok[:W, :], src_ap, [], [xtok], "xtok")
            for half in range(2):
                pb = bank()
                for c4 in range(4):
                    c = half * 4 + c4
                    TR(pb[:, c4 * W:(c4 + 1) * W], xtok[:W, c * 128:(c + 1) * 128], ident[:W, :W], [xtok, cst], pb)
                CP(XT_()[:, half * 4:(half + 1) * 4, :W], pb[:, 0:4 * W].rearrange("p (c w) -> p c w", w=W), [pb], [XT_()],
                   eng="act")

        def l0_mixer(W, cl):
            nch = W // cl
            rms_rstd(W)
            for c in range(8):
                STT(hT[:, c, :W], XT_()[:, c, :W], pcol("n1g0", c), rstd[:, :W], ALU.mult, ALU.mult, [XT_(), pc, rstd], [hT])

            L0L = int(os.environ.get('K_L0', '9'))
            if L0L < 2:
                return

            def proj_fm(g, off=0, nchunk=4):
                wb_ = wload(g)
                pb = bank()
                for j in range(nchunk):
                    for c in range(8):
                        MM(pb[:, j * W:(j + 1) * W], wb_[:, c, off + j * 128:off + (j + 1) * 128], hT[:, c, :W],
                           c == 0, c == 7, [wb_, hT], pb)
                return pb

            def proj_tm(g, dst, eng):
                wb_ = wload(g)
                pb = bank()
                for c in range(8):
                    MM(pb[:W, :512], hT[:, c, :W], wb_[:, c, :], c == 0, c == 7, [wb_, hT], pb)
                CP(dst[:W, :], pb[:W, :], [pb], [dst], eng=eng)

            qA, fA, lfA, kA, gA, bq, bk, gB, lgB = Fb[2], Fb[3], Fb[4], Fb[5], Fb[6], Fb[7], Fb[8], Fb[9], Fb[10]
            pb = proj_fm(G_IN["a_q"])
            ACT(qA[:, :4 * W], pb[:, :4 * W], AF.Silu, [pb], [qA])
            pb = proj_fm(G_IN["a_f"])
            ACT(fA[:, :4 * W], pb[:, :4 * W], AF.Sigmoid, [pb], [fA])
            for h in range(4):
                TS(fA[:, h * W:(h + 1) * W], fA[:, h * W:(h + 1) * W], dcol[:, 4 + h:5 + h], dcol[:, h:h + 1],
                   ALU.mult, ALU.add, [fA, dcol], [fA])
            ACT(lfA[:, :4 * W], fA[:, :4 * W], AF.Ln, [fA], [lfA])
            TS(kA[:, :4 * W], fA[:, :4 * W], -1.0, 1.0, ALU.mult, ALU.add, [fA], [kA])
            vA, vB = Hb[1], Hb[2]
            proj_tm(G_IN["a_i"], vA, "act")
            pb = proj_fm(G_IN["a_gate"])
            ACT(gA[:, :4 * W], pb[:, :4 * W], AF.Silu, [pb], [gA])
            pb = proj_fm(G_IN["b_qk"])
            CP(bq[:, :2 * W], pb[:, :2 * W], [pb], [bq], eng="act")
            CP(bk[:, :2 * W], pb[:, 2 * W:4 * W], [pb], [bk])
            proj_tm(G_IN["b_v"], vB, "dve")
            pb = proj_fm(G_IN["b_gate"])
            ACT(gB[:, :4 * W], pb[:, :4 * W], AF.Silu, [pb], [gB])
            blr = blrb
            pb = bank()
            for c in range(8):
                MM(pb[:, :W], WLR[:, c, :], hT[:, c, :W], c == 0, c == 7, [WLR, hT], pb)
            CP(blr[:, :W], pb[:, :W], [pb], [blr])
            pb = bank()
            for j in range(2):
                MM(pb[:, j * W:(j + 1) * W], GW2[:, j * 128:(j + 1) * 128], blr[:, :W], True, True, [GW2, blr], pb)
            for j in range(2):
                ACT(lgB[:, j * W:(j + 1) * W], pb[:, j * W:(j + 1) * W], AF.Exp, [pb, dcol], [lgB],
                    bias=dcol[:, 8 + j:9 + j], scale=-1.0)
            ACT(lgB[:, :2 * W], lgB[:, :2 * W], AF.Ln, [lgB, eps_col], [lgB], bias=eps_col[:, 1:2], scale=1.0)
            TS(lgB[:, :2 * W], lgB[:, :2 * W], -1.0 / 16.0, None, ALU.mult, None, [lgB], [lgB])

            if L0L < 3:
                return
            rst = rst64 if cl == 64 else rst32
            bcA, ebA, eblA = Fb[11], Fb[0], Fb[1]
            for h in range(4):
                SCAN(bcA[:, h * W:(h + 1) * W], rst[:, :W], lfA[:, h * W:(h + 1) * W], [cst, lfA], [bcA])
            ACT(ebA[:, :4 * W], bcA[:, :4 * W], AF.Exp, [bcA], [ebA])
            qtA, ktA, qtB, ktB = Hb[3], Hb[4], Hb[5], Hb[6]
            TT(qtA[:, :4 * W], qA[:, :4 * W], ebA[:, :4 * W], ALU.mult, [qA, ebA], [qtA])
            for ci in range(nch):
                CP(eblA[:, ci * 4:ci * 4 + 4], ebA[:, 0:4 * W].rearrange("p (h w) -> p h w", w=W)[:, :, ci * cl + cl - 1],
                   [ebA], [eblA])
            ACT(ebA[:, :4 * W], bcA[:, :4 * W], AF.Exp, [bcA], [ebA], scale=-1.0)
            TT(ktA[:, :4 * W], kA[:, :4 * W], ebA[:, :4 * W], ALU.mult, [kA, ebA], [ktA])
            bcB, ebB = Fb[2], Fb[3]
            for j in range(2):
                SCAN(bcB[:, j * W:(j + 1) * W], rst[:, :W], lgB[:, j * W:(j + 1) * W], [cst, lgB], [bcB])
            ACT(ebB[:, :2 * W], bcB[:, :2 * W], AF.Exp, [bcB], [ebB])
            STT(qtB[:, :2 * W], bq[:, :2 * W], 0.125, ebB[:, :2 * W], ALU.mult, ALU.mult, [bq, ebB], [qtB])
            for ci in range(nch):
                CP(eblA[:, 16 + ci * 2:16 + ci * 2 + 2],
                   ebB[:, 0:2 * W].rearrange("p (h w) -> p h w", w=W)[:, :, ci * cl + cl - 1], [ebB], [eblA])
            ACT(ebB[:, :2 * W], bcB[:, :2 * W], AF.Exp, [bcB], [ebB], scale=-1.0)
            TT(ktB[:, :2 * W], bk[:, :2 * W], ebB[:, :2 * W], ALU.mult, [bk, ebB], [ktB])

            if L0L < 4:
                return
            atA, atB = Hb[7], Hb[8]
            pb = bank()
            for h in range(4):
                MM(pb[:W, h * W:(h + 1) * W], ktA[:, h * W:(h + 1) * W], qtA[:, h * W:(h + 1) * W], True, True, [ktA, qtA], pb)
            TT(atA[:W, :4 * W].rearrange("p (h w) -> p h w", w=W), pb[:W, :4 * W].rearrange("p (h w) -> p h w", w=W),
               cmask[:W, :W].unsqueeze(1).broadcast_to([W, 4, W]), ALU.mult, [pb, cst], [atA])
            qtBx = Hb[0]
            for h in range(4):
                TS(qtBx[:, h * W:(h + 1) * W], qtB[:, (h // 2) * W:(h // 2 + 1) * W], hmask[h % 2], None, ALU.mult, None,
                   [qtB, cst], [qtBx])
            pb = bank()
            for h in range(4):
                j = h // 2
                MM(pb[:W, h * W:(h + 1) * W], ktB[:, j * W:(j + 1) * W], qtBx[:, h * W:(h + 1) * W], True, True, [ktB, qtBx], pb)
            TT(atB[:W, :4 * W].rearrange("p (h w) -> p h w", w=W), pb[:W, :4 * W].rearrange("p (h w) -> p h w", w=W),
               cmask[:W, :W].unsqueeze(1).broadcast_to([W, 4, W]), ALU.mult, [pb, cst], [atB])
            kTokA, kTokB = Hb[9], Hb[10]
            pb = bank()
            for h in range(4):
                MM(pb[:W, h * 128:(h + 1) * 128], ktA[:, h * W:(h + 1) * W], identb, True, True, [ktA, cstb], pb)
            CP(kTokA[:W, :512], pb[:W, :512], [pb], [kTokA], eng="act")
            pb = bank()
            for j in range(2):
                MM(pb[:W, j * 128:(j + 1) * 128], ktB[:, j * W:(j + 1) * W], identb, True, True, [ktB, cstb], pb)
            CP(kTokB[:W, :256], pb[:W, :256], [pb], [kTokB])

            if L0L < 5:
                return

            vAx, vBx = [vA], [vB]
            if nch == 2:
                vAx, vBx = [Hb[11], Hb[12]], [ooTA, ooTB]
                for ci in range(2):
                    TS(vAx[ci][:W, :], vA[:W, :], hmask[ci][:W], None, ALU.mult, None, [vA, cst], [vAx[ci]])
                    TS(vBx[ci][:W, :], vB[:W, :], hmask[ci][:W], None, ALU.mult, None, [vB, cst], [vBx[ci]])

            def update(ci, dstA, dstB):
                pu_ = bank()
                for h in range(4):
                    MM(pu_[:, h * 128:(h + 1) * 128], kTokA[:W, h * 128:(h + 1) * 128], vAx[ci][:W, h * 128:(h + 1) * 128],
                       True, True, [kTokA, vAx[ci]], pu_)
                TT(Sa[:].rearrange("p h v -> p (h v)"), Sa[:].rearrange("p h v -> p (h v)"), pu_[:, :512], ALU.add, [Sa, pu_], [Sa])
                TT(Sa[:], Sa[:], eblA[:, ci * 4:ci * 4 + 4].unsqueeze(2).broadcast_to([128, 4, 128]), ALU.mult,
                   [Sa, eblA], [Sa])
                CP(dstA[:], Sa[:], [Sa], [dstA], eng="act")
                pg_ = bank()
                for j in range(2):
                    MM(pg_[:, j * 256:(j + 1) * 256], kTokB[:W, j * 128:(j + 1) * 128], vBx[ci][:W, j * 256:(j + 1) * 256],
                       True, True, [kTokB, vBx[ci]], pg_)
                for hh in range(2):
                    ps_ = slice(hh * 64, hh * 64 + 64)
                    TT(Sg[ps_], Sg[ps_], pg_[ps_, 0:512].rearrange("p (j x v) -> p j x v", j=2, x=2)[:, :, hh, :], ALU.add,
                       [Sg, pg_], [Sg])
                TT(Sg[:], Sg[:], eblA[:, 16 + ci * 2:16 + ci * 2 + 2].unsqueeze(2).broadcast_to([128, 2, 128]),
                   ALU.mult, [Sg, eblA], [Sg])
                CP(dstB[:], Sg[:], [Sg], [dstB], eng="act")

            for ci in range(nch - 1):
                update(ci, SabC[ci + 1], SgbC[ci + 1])
            poA = bank()
            poB = bank()
            for h in range(4):
                MM(poA[:, h * W:(h + 1) * W], vA[:W, h * 128:(h + 1) * 128], atA[:W, h * W:(h + 1) * W], True, False,
                   [vA, atA], poA)
                for ci in range(nch):
                    MM(poA[:, h * W + ci * cl:h * W + (ci + 1) * cl], SabC[ci][:, h, :],
                       qtA[:, h * W + ci * cl:h * W + (ci + 1) * cl], False, ci == nch - 1, [SabC[ci], qtA], poA)
            for h in range(4):
                j = h // 2
                MM(poB[:, h * W:(h + 1) * W], vB[:W, h * 128:(h + 1) * 128], atB[:W, h * W:(h + 1) * W], True, False,
                   [vB, atB], poB)
                for ci in range(nch):
                    MM(poB[:, h * W + ci * cl:h * W + (ci + 1) * cl], SgbC[ci][:, j, :],
                       qtBx[:, h * W + ci * cl:h * W + (ci + 1) * cl], False, ci == nch - 1, [SgbC[ci], qtBx], poB)
            update(nch - 1, SabC[0], SgbC[0])

            if L0L < 6:
                return
            for (po, gbuf, gname, oo) in ((poA, gA, "hng", ooTA), (poB, gB, "gng", ooTB)):
                sq, rs = Fb[4], Fb[5]
                ACT(sq[:, :4 * W], po[:, :4 * W], AF.Square, [po], [sq])
                pn = bank()
                MM(pn[:, :4 * W], ones, sq[:, :4 * W], True, True, [cst, sq], pn)
                ACT(rs[:, :4 * W], pn[:, :4 * W], AF.Sqrt, [pn, eps_col], [rs], bias=eps_col[:, 0:1], scale=1.0 / 128.0)
                RECIP(rs[:, :4 * W], rs[:, :4 * W], [rs], [rs])
                TT(rs[:, :4 * W], rs[:, :4 * W], po[:, :4 * W], ALU.mult, [rs, po], [rs])
                STT(oo[:, :4 * W], rs[:, :4 * W], pcol(gname), gbuf[:, :4 * W], ALU.mult, ALU.mult, [rs, pc, gbuf], [oo])
            if L0L < 7:
                return
            for half in range(2):
                wb_ = wload(G_OAB + half)
                pb = bank()
                for j in range(4):
                    for c in range(8):
                        oo = ooTA if c < 4 else ooTB
                        MM(pb[:, j * W:(j + 1) * W], wb_[:, c, j * 128:(j + 1) * 128], oo[:, (c % 4) * W:(c % 4 + 1) * W],
                           c == 0, c == 7, [wb_, oo], pb)
                TT(XT_()[:, half * 4:(half + 1) * 4, :W], XT_()[:, half * 4:(half + 1) * 4, :W],
                   pb[:, :4 * W].rearrange("p (c w) -> p c w", w=W), ALU.add, [XT_(), pb], [XT_()])


        def flat(buf):
            t = buf.t
            nd = len(t.shape)
            if nd == 2:
                return buf[:, :]
            if nd == 3:
                return buf[:].rearrange("p a b -> p (a b)")
            return buf[:].rearrange("p a b c -> p (a b c)")

        def l1_mixer(W, cl):
            nch = W // cl
            n = 2 * cl
            J = 6 if cl == 64 else 5
            mo = 0 if cl == 64 else 640
            M1 = cst2[:n, mo:mo + n + cl]
            M2 = cst2[:n, mo + (n + cl):mo + 2 * (n + cl)]
            M3 = cst2[:n, mo + 2 * (n + cl):mo + 2 * (n + cl) + n]
            BMt = cst2[:n, mo + 2 * (n + cl) + n:mo + 2 * (n + cl) + n + 128]
            BMf2 = cst[:, 384:512:64]

            def arr(buf, off):
                return flat(buf)[:, off:off + 8 * W].rearrange("p (c w) -> p c w", w=W)

            def arrf(buf, off):
                return flat(buf)[:, off:off + 8 * W]
            rB, kB_, vB_, aB, lwB, oB, kkB, beB, eCB, enB = scH[0], scH[1], sc2H[0], sc2H[1], t4H[0], t4H[1], junk, acc, h2tok, xtok
            rA, kA_, vA_, aA, lwA, oA = arr(scH[0], 0), arr(scH[1], 0), arr(sc2H[0], 0), arr(sc2H[1], 0), arr(t4H[0], 0), arr(t4H[1], 0)
            kkA, beA, eCA, enA = arr(junk, 0), arr(acc, 0), arr(h2tok, 0), arr(xtok, 0)
            rF, kF, vF, aF, lwF, oF = arrf(scH[0], 0), arrf(scH[1], 0), arrf(sc2H[0], 0), arrf(sc2H[1], 0), arrf(t4H[0], 0), arrf(t4H[1], 0)
            kkF, beF, eCF, enF = arrf(junk, 0), arrf(acc, 0), arrf(h2tok, 0), arrf(xtok, 0)
            bonH = (Fb[6], Fb[7])
            gH = (Fb[8], Fb[9])

            def pcb(name, i=0):
                c0 = PCOL[name] + i * 8
                return pc[:, c0:c0 + 8].unsqueeze(2).broadcast_to([128, 8, W])

            rms_rstd(W)
            for c in range(8):
                STT(HS[:, c, 1:W + 1], XT_()[:, c, :W], pcol("n1g1", c), rstd[:, :W], ALU.mult, ALU.mult, [XT_(), pc, rstd], [HS])
            dxA = oA
            TT(dxA, HS[:, :, 0:W], HS[:, :, 1:W + 1], ALU.subtract, [HS], [oB])
            mixbufs = (hT, qT)

            def make_mix(i):
                mb_ = mixbufs[i % 2]
                mv = flat(mb_)[:, 0:8 * W].rearrange("p (c w) -> p c w", w=W)
                TT(eCA, dxA, pcb("mu", i), ALU.mult, [oB, pc], [eCB])
                TT(mv, eCA, HS[:, :, 1:W + 1], ALU.add, [eCB, HS], [mb_])
                return mb_, mv

            def proj_full(mb_, mv, g0, dstF, dstB):
                for half in range(2):
                    wb_ = wload(g0 + half)
                    pb = bank()
                    for j in range(4):
                        for c in range(8):
                            MM(pb[:, j * W:(j + 1) * W], wb_[:, c, j * 128:(j + 1) * 128], mv[:, c, :], c == 0, c == 7, [wb_, mb_], pb)
                    CP(dstF[:, half * 4 * W:(half + 1) * 4 * W], pb[:, :4 * W], [pb], [dstB], eng=("act" if half == 0 else "dve"))

            mb_, mv = make_mix(0)
            proj_full(mb_, mv, G_RKV + 0, rF, rB)
            mb_, mv = make_mix(1)
            proj_full(mb_, mv, G_RKV + 2, kF, kB_)
            mb_, mv = make_mix(2)
            proj_full(mb_, mv, G_RKV + 4, vF, vB_)
            lrb = Hb[3]
            for (i, c0, rows, func, W2_, dstA, dstB, bname) in ((3, 0, 64, AF.Tanh, W2w, lwA, lwB, "w0"),
                                                                  (4, 64, 64, AF.Copy, W2a, aA, aB, "a0")):
                mb_, mv = make_mix(i)
                pb = bank()
                for c in range(8):
                    MM(pb[:rows, :W], W1r[:, c, c0:c0 + rows], mv[:, c, :], c == 0, c == 7, [W1r, mb_], pb)
                ACT(lrb[:rows, :W], pb[:rows, :W], func, [pb], [lrb])
                for half in range(2):
                    pb = bank()
                    for j in range(4):
                        oc = half * 4 + j
                        MM(pb[:, j * W:(j + 1) * W], W2_[0:rows, oc * 128:(oc + 1) * 128], lrb[0:rows, :W], True, True, [W2_, lrb], pb)
                    for j in range(4):
                        oc = half * 4 + j
                        ACT(dstA[:, oc, :], pb[:, j * W:(j + 1) * W], AF.Sigmoid, [pb, pc], [dstB], bias=pcol(bname, oc), scale=1.0)
            mb_, mv = make_mix(5)
            pb = bank()
            for c in range(8):
                MM(pb[:, :W], W1r[:, c, 128:256], mv[:, c, :], c == 0, c == 7, [W1r, mb_], pb)
            ACT(lrb[:, :W], pb[:, :W], AF.Sigmoid, [pb], [lrb])
            for half in range(2):
                pb = bank()
                for j in range(4):
                    oc = half * 4 + j
                    MM(pb[:, j * W:(j + 1) * W], W2g[:, oc * 128:(oc + 1) * 128], lrb[:, :W], True, True, [W2g, lrb], pb)
                CP(gH[half][:, :4 * W], pb[:, :4 * W], [pb], [gH[half]], eng="act")

            TT(kkA, kA_, pcb("kk"), ALU.mult, [kB_, pc], [kkB])
            ACT(eCF, kkF, AF.Square, [kkB], [eCB])
            for half in range(2):
                pb = bank()
                MM(pb[:, :4 * W], cst[:, 384:512], eCF[:, half * 4 * W:(half + 1) * 4 * W], True, True, [cst, eCB], pb)
                ACT(enF[:, half * 4 * W:(half + 1) * 4 * W], pb[:, :4 * W], AF.Sqrt, [pb, eps_col], [enB], bias=eps_col[:, 2:3], scale=1.0)
            RECIP(enF, enF, [enB], [enB])
            TT(kkF, kkF, enF, ALU.mult, [kkB, enB], [kkB])
            TT(eCA, aA, pcb("ka"), ALU.mult, [aB, pc], [eCB])
            TT(eCA, eCA, dcol2[:, 0:8].unsqueeze(2).broadcast_to([128, 8, W]), ALU.add, [eCB, dcol2], [eCB])
            TT(kF, kF, eCF, ALU.mult, [kB_, eCB], [kB_])
            TT(beF, kkF, aF, ALU.mult, [kkB, aB], [beB])
            TT(eCF, rF, kF, ALU.mult, [rB, kB_], [eCB])
            TT(eCA, eCA, pcb("rk"), ALU.mult, [eCB, pc], [eCB])
            for half in range(2):
                pb = bank()
                MM(pb[:, :4 * W], cst[:, 384:512], eCF[:, half * 4 * W:(half + 1) * 4 * W], True, True, [cst, eCB], pb)
                TT(bonH[half][:, :4 * W], pb[:, :4 * W], vF[:, half * 4 * W:(half + 1) * 4 * W], ALU.mult, [pb, vB_], [bonH[half]])
            rst = rst64 if cl == 64 else rst32
            TS(lwF, lwF, -0.6065306597126334, None, ALU.mult, None, [lwB], [lwB])
            for c in range(8):
                SCAN(aA[:, c, :], rst[:, :W], lwA[:, c, :], [cst, lwB], [aB])
            TT(enF, aF, lwF, ALU.subtract, [aB, lwB], [enB])
            ACT(enF, enF, AF.Exp, [enB], [enB])
            TT(kkF, kkF, enF, ALU.mult, [kkB, enB], [kkB])
            ACT(eCF, aF, AF.Exp, [aB], [eCB])
            TT(rF, rF, eCF, ALU.mult, [rB, eCB], [rB])
            for ci in range(nch):
                CP(GC[:, :, ci], eCA[:, :, ci * cl + cl - 1], [eCB], [GC])
            ACT(enF, aF, AF.Exp, [aB], [enB], scale=-1.0)
            TT(beF, beF, enF, ALU.mult, [beB, enB], [beB])
            TT(kF, kF, enF, ALU.mult, [kB_, enB], [kB_])

            def bfv(buf, off, shape4):
                v = flat(buf).bitcast(BF16)
                tot = 1
                for x in shape4[1:]:
                    tot *= x
                v = v[:shape4[0], off:off + tot]
                if len(shape4) == 3:
                    return v.rearrange("p (a b) -> p a b", b=shape4[2])
                return v.rearrange("p (a b c) -> p a b c", b=shape4[2], c=shape4[3])
            wslot = n + cl
            RK = flat(mixbufs[1])[:, 0:8 * wslot].rearrange("p (c x) -> p c x", x=wslot)
            BB = stageb[:, 0:8 * n].rearrange("p (c x) -> p c x", x=n)
            KB = stageb[:, 1024:1024 + 8 * n].rearrange("p (c x) -> p c x", x=n)
            VB = flat(hT)[:, 0:8 * n].rearrange("p (c x) -> p c x", x=n)
            SB1 = bfv(slots[0], 0, (n, 8, wslot))
            SB2 = bfv(slots[1], 0, (n, 8, wslot))
            Ybuf = [(bfv(slots[2], 0, (n, 8, n)), bfv(slots[2], 1024, (n, 8, n)), slots[2]),
                    (bfv(slots[3], 0, (n, 8, n)), bfv(slots[3], 1024, (n, 8, n)), slots[3])]
            Ublk = bfv(Fb[0], 0, (n, 8, 128))
            Vblk = bfv(Fb[1], 0, (n, 8, 128))
            BT = bfv(Fb[2], 0, (n, 8, 128))
            KT = bfv(Fb[3], 0, (n, 8, 128))
            P0blk = bfv(Fb[5], 0, (128, 8, 128))
            Uf, Ub, Vb, P0b = Fb[4], Hb[0], Hb[1], Hb[2]
            bm4 = BMf2.unsqueeze(1).unsqueeze(3).broadcast_to([128, 8, 2, cl])

            def refresh_P():
                CP(P0b[:, :512], flat(Pst), [Pst], [P0b], eng="act")
                TT(P0blk.rearrange("p c (h i) -> p c h i", h=2), Pst[:].unsqueeze(2).broadcast_to([128, 8, 2, 64]),
                   BMf2.unsqueeze(1).unsqueeze(3).broadcast_to([128, 8, 2, 64]), ALU.mult, [Pst, cst], [Fb[5]])
            refresh_P()
            for ci in range(nch):
                cs_ = slice(ci * cl, (ci + 1) * cl)

                def blk(dst, src):
                    return (dst.rearrange("p c (h s) -> p c h s", h=2), src[:, :, cs_].unsqueeze(2).broadcast_to([128, 8, 2, cl]))
                d_, s_ = blk(RK[:, :, 0:n], kkA)
                TT(d_, s_, bm4, ALU.mult, [kkB, cst], [mixbufs[1]])
                CP(RK[:, :, n:n + cl], rA[:, :, cs_], [rB], [mixbufs[1]], eng="act")
                d_, s_ = blk(BB, beA)
                TT(d_, s_, bm4, ALU.mult, [beB, cst], [stageb])
                d_, s_ = blk(KB, kA_)
                TT(d_, s_, bm4, ALU.mult, [kB_, cst], [stageb], eng="pool")
                d_, s_ = blk(VB, vA_)
                TT(d_, s_, bm4, ALU.mult, [vB_, cst], [hT], eng="pool")
                for (lh, SB, Mk, sbuf_) in ((BB, SB1, M1, slots[0]), (KB, SB2, M2, slots[1])):
                    for c2 in range(4):
                        pb = bank()
                        for u in range(2):
                            c = c2 * 2 + u
                            MM(pb[:n, u * wslot:(u + 1) * wslot], lh[:, c, :], RK[:, c, :], True, True, [stageb, mixbufs[1]], pb)
                        TT(SB[:, c2 * 2:c2 * 2 + 2, :], pb[:n, 0:2 * wslot].rearrange("p (u x) -> p u x", u=2),
                           Mk.unsqueeze(1).broadcast_to([n, 2, wslot]), ALU.mult, [pb, cst2], [sbuf_])
                Y0, YT0, yb0 = Ybuf[0]
                npb = 512 // n
                for c4 in range(0, 8, npb):
                    pb = bank()
                    for u in range(npb):
                        c = c4 + u
                        MM(pb[:n, u * n:(u + 1) * n], RK[:, c, 0:n], BB[:, c, :], True, True, [mixbufs[1], stageb], pb)
                    TT(Y0[:, c4:c4 + npb, :], pb[:n, 0:npb * n].rearrange("p (u x) -> p u x", u=npb),
                       M3.unsqueeze(1).broadcast_to([n, npb, n]), ALU.mult, [pb, cst2], [yb0])
                pb = bank()
                for c in range(8):
                    MM(pb[:n, c * 64:(c + 1) * 64], VB[:, c, :], E2b[:, :], True, True, [hT, E2b], pb)
                CP(Vb[:n, :512], pb[:n, :512], [pb], [Vb], eng="act")
                pb = bank()
                for c in range(8):
                    MM(pb[:n, c * 64:(c + 1) * 64], RK[:, c, 0:n], P0b[:, c * 64:(c + 1) * 64], True, False, [mixbufs[1], P0b], pb)
                    MM(pb[:n, c * 64:(c + 1) * 64], SB2[:, c, 0:n], Vb[:n, c * 64:(c + 1) * 64], False, True, [slots[1], Vb], pb)
                TS(Uf[:n, :512], pb[:n, :512], -1.0, None, ALU.mult, None, [pb], [Uf])
                CP(Ub[:n, :512], Uf[:n, :512], [Uf], [Ub], eng="act")
                curY, curYT, curYb, curYTb = Y0, SB1[:, :, 0:n], yb0, slots[0]
                for j in range(J):
                    pb = bank()
                    for c in range(8):
                        MM(pb[:n, c * 64:(c + 1) * 64], curYT[:, c, :], Ub[:n, c * 64:(c + 1) * 64], True, True, [curYTb, Ub], pb)
                    TT(Uf[:n, :512], Uf[:n, :512], pb[:n, :512], ALU.add, [Uf, pb], [Uf])
                    CP(Ub[:n, :512], Uf[:n, :512], [Uf], [Ub], eng="act")
                    if j < J - 1:
                        nY, nYT, nYb = Ybuf[(j + 1) % 2]
                        for c4 in range(0, 8, npb):
                            pb1 = bank()
                            pb2 = bank()
                            for u in range(npb):
                                c = c4 + u
                                MM(pb1[:n, u * n:(u + 1) * n], curYT[:, c, :], curY[:, c, :], True, True, [curYTb, curYb], pb1)
                                MM(pb2[:n, u * n:(u + 1) * n], curY[:, c, :], curYT[:, c, :], True, True, [curYTb, curYb], pb2)
                            CP(nY[:, c4:c4 + npb, :], pb1[:n, 0:npb * n].rearrange("p (u x) -> p u x", u=npb), [pb1], [nYb], eng="act")
                            CP(nYT[:, c4:c4 + npb, :], pb2[:n, 0:npb * n].rearrange("p (u x) -> p u x", u=npb), [pb2], [nYb])
                        curY, curYT, curYb, curYTb = nY, nYT, nYb, nYb
                bmt4 = BMt.rearrange("p (h i) -> p h i", h=2).unsqueeze(1).broadcast_to([n, 8, 2, 64])
                TT(Ublk.rearrange("p c (h i) -> p c h i", h=2),
                   Ub[:n, :512].rearrange("p (c i) -> p c i", i=64).unsqueeze(2).broadcast_to([n, 8, 2, 64]), bmt4, ALU.mult,
                   [Ub, cst2], [Fb[0]])
                TT(Vblk.rearrange("p c (h i) -> p c h i", h=2),
                   Vb[:n, :512].rearrange("p (c i) -> p c i", i=64).unsqueeze(2).broadcast_to([n, 8, 2, 64]), bmt4, ALU.mult,
                   [Vb, cst2], [Fb[1]], eng="pool")
                pb = bank()
                for c in range(8):
                    MM(pb[:, c * cl:(c + 1) * cl], P0blk[:, c, :], RK[:, c, n:n + cl], True, False, [Fb[5], mixbufs[1]], pb)
                    MM(pb[:, c * cl:(c + 1) * cl], Ublk[:, c, :], SB1[:, c, n:n + cl], False, False, [Fb[0], slots[0]], pb)
                    MM(pb[:, c * cl:(c + 1) * cl], Vblk[:, c, :], SB2[:, c, n:n + cl], False, True, [Fb[1], slots[1]], pb)
                CP(oA[:, :, cs_], pb[:, 0:8 * cl].rearrange("p (c t) -> p c t", t=cl), [pb], [oB], eng="act")
                for (src, dstv, dbuf, eng) in ((BB, BT, Fb[2], "act"), (KB, KT, Fb[3], "dve")):
                    for c4 in range(0, 8, 4):
                        pb = bank()
                        for u in range(4):
                            MM(pb[:n, u * 128:(u + 1) * 128], src[:, c4 + u, :], identb, True, True, [stageb, cstb], pb)
                        CP(dstv[:, c4:c4 + 4, :], pb[:n, :512].rearrange("p (u x) -> p u x", u=4), [pb], [dbuf], eng=eng)
                pb = bank()
                for c in range(8):
                    MM(pb[:, c * 64:(c + 1) * 64], BT[:, c, :], Ub[:n, c * 64:(c + 1) * 64], True, False, [Fb[2], Ub], pb)
                    MM(pb[:, c * 64:(c + 1) * 64], KT[:, c, :], Vb[:n, c * 64:(c + 1) * 64], False, True, [Fb[3], Vb], pb)
                TT(flat(Pst), flat(Pst), pb[:, :512], ALU.add, [Pst, pb], [Pst])
                TT(Pst[:], Pst[:], GC[:, :, ci].unsqueeze(2).broadcast_to([128, 8, 64]), ALU.mult, [Pst, GC], [Pst])
                if ci < nch - 1:
                    refresh_P()

            for half in range(2):
                oh = oF[:, half * 4 * W:(half + 1) * 4 * W]
                dd, sq, rs = Fb[0], Fb[1], Fb[2]
                pb = bank()
                MM(pb[:, :4 * W], cst[:, 384:512], oh, True, True, [cst, oB], pb)
                STT(dd[:, :4 * W], pb[:, :4 * W], -1.0 / 64.0, oh, ALU.mult, ALU.add, [pb, oB], [dd])
                ACT(sq[:, :4 * W], dd[:, :4 * W], AF.Square, [dd], [sq])
                pb = bank()
                MM(pb[:, :4 * W], cst[:, 384:512], sq[:, :4 * W], True, True, [cst, sq], pb)
                ACT(rs[:, :4 * W], pb[:, :4 * W], AF.Sqrt, [pb, eps_col], [rs], bias=eps_col[:, 3:4], scale=1.0 / 64.0)
                RECIP(rs[:, :4 * W], rs[:, :4 * W], [rs], [rs])
                TT(dd[:, :4 * W], dd[:, :4 * W], rs[:, :4 * W], ALU.mult, [dd, rs], [dd])
                for j in range(4):
                    c = half * 4 + j
                    TS(dd[:, j * W:(j + 1) * W], dd[:, j * W:(j + 1) * W], pcol("lng", c), pcol("lnb", c), ALU.mult, ALU.add,
                       [dd, pc], [dd])
                TT(dd[:, :4 * W], dd[:, :4 * W], bonH[half][:, :4 * W], ALU.add, [dd, bonH[half]], [dd])
                oo = ooTA if half == 0 else ooTB
                TT(oo[:, :4 * W], dd[:, :4 * W], gH[half][:, :4 * W], ALU.mult, [dd, gH[half]], [oo])
            for half in range(2):
                wb_ = wload(G_OC + half)
                pb = bank()
                for j in range(4):
                    for c in range(8):
                        oo = ooTA if c < 4 else ooTB
                        MM(pb[:, j * W:(j + 1) * W], wb_[:, c, j * 128:(j + 1) * 128], oo[:, (c % 4) * W:(c % 4 + 1) * W],
                           c == 0, c == 7, [wb_, oo], pb)
                TT(XT_()[:, half * 4:(half + 1) * 4, :W], XT_()[:, half * 4:(half + 1) * 4, :W],
                   pb[:, :4 * W].rearrange("p (c w) -> p c w", w=W), ALU.add, [XT_(), pb], [XT_()])
            CP(HS[:, :, 0], HS[:, :, W], [HS], [HS], eng="act")

        def peer(layer, W, final_out=None):
            gname = "n2g0" if layer == 0 else "n2g1"
            h2b = (Hb[11], Hb[12])
            h2f = (Fb[2], Fb[3])
            rms_rstd(W)
            for c in range(8):
                dstf = h2f[c // 4][:, (c % 4) * W:(c % 4 + 1) * W]
                STT(dstf, XT_()[:, c, :W], pcol(gname, c), rstd[:, :W], ALU.mult, ALU.mult, [XT_(), pc, rstd], [h2f[c // 4]])
            for half in range(2):
                CP(h2b[half][:, :4 * W], h2f[half][:, :4 * W], [h2f[half]], [h2b[half]], eng="act")
            for half in range(2):
                pb = bank()
                for c4 in range(4):
                    TR(pb[:W, c4 * 128:(c4 + 1) * 128], h2f[half][:, c4 * W:(c4 + 1) * W], ident, [h2f[half], cst], pb)
                CP(h2tok[:W, half * 512:(half + 1) * 512], pb[:W, :512], [pb], [h2tok], eng="act")
            for g4 in range(4):
                wb_ = wload(G_WQ + layer * 4 + g4)
                pb = bank()
                for j in range(4):
                    for c in range(8):
                        src = h2b[c // 4]
                        MM(pb[:, j * W:(j + 1) * W], wb_[:, c, j * 128:(j + 1) * 128], src[:, (c % 4) * W:(c % 4 + 1) * W],
                           c == 0, c == 7, [wb_, src], pb)
                CP(qT[:, g4 * 4:(g4 + 1) * 4, :W], pb[:, :4 * W].rearrange("p (g w) -> p g w", w=W), [pb], [qT],
                   eng=("act" if g4 % 2 == 0 else "dve"))
            for g4 in range(4):
                pb = bank()
                for j in range(4):
                    g = g4 * 4 + j
                    MM(pb[:W, j * 128:(j + 1) * 128], qT[:, g, :W], SK[:, layer, g, :], True, True, [qT, SK], pb)
                CP(scH[g4 // 2][:W, (g4 % 2) * 4:(g4 % 2) * 4 + 4, :], pb[:W, :512].rearrange("p (g n) -> p g n", n=128), [pb],
                   [scH[g4 // 2]], eng="act")
            def sub_trackers(parent, n, tag):
                subs = []
                for i_ in range(n):
                    b_ = Buf(parent.t, f"{parent.name}_{tag}{i_}")
                    b_.lw = parent.lw
                    b_.rd = dict(parent.rd)
                    subs.append(b_)
                return subs

            def join_trackers(parent, subs):
                lw = parent.lw
                for b_ in subs:
                    for kk_, vv_ in list(b_.rd.items()) + ([b_.lw] if b_.lw else []):
                        if parent.rd.get(kk_, 0) < vv_:
                            parent.rd[kk_] = vv_
                    if b_.lw is not None and b_.lw[0] == "c_dve" and (lw is None or lw[0] != "c_dve" or lw[1] < b_.lw[1]):
                        lw = b_.lw
                parent.lw = lw

            scG = sub_trackers(scH[0], 8, "g") + sub_trackers(scH[1], 8, "g")
            sc2G = sub_trackers(sc2H[0], 8, "g") + sub_trackers(sc2H[1], 8, "g")
            svG = sub_trackers(sv, 16, "g")
            siuG = sub_trackers(siu, 16, "g")
            for stage in range(5):
                for g in range(16):
                    sb1, sb2, gi = scH[g // 8], sc2H[g // 8], g % 8
                    t1, t2, tv, ti = scG[g], sc2G[g], svG[g], siuG[g]
                    if stage == 0:
                        P.op("dve", lambda e, g=g, sb1=sb1, gi=gi: e.max(out=sv[:W, g, 0:8], in_=sb1[:W, gi, :]), reads=[t1], writes=[tv])
                    elif stage == 1:
                        P.op("dve", lambda e, g=g, sb1=sb1, gi=gi: e.max_index(out=siu[:W, g, 0:8], in_max=sv[:W, g, 0:8],
                                                                               in_values=sb1[:W, gi, :]), reads=[t1, tv], writes=[ti])
                    elif stage == 2:
                        P.op("dve", lambda e, g=g, sb1=sb1, sb2=sb2, gi=gi: e.match_replace(
                            out=sb2[:W, gi, :], in_to_replace=sv[:W, g, 0:8], in_values=sb1[:W, gi, :], imm_value=NEG),
                            reads=[t1, tv], writes=[t2])
                    elif stage == 3:
                        P.op("dve", lambda e, g=g, sb2=sb2, gi=gi: e.max(out=sv[:W, g, 8:16], in_=sb2[:W, gi, :]), reads=[t2], writes=[tv])
                    else:
                        P.op("dve", lambda e, g=g, sb2=sb2, gi=gi: e.max_index(out=siu[:W, g, 8:16], in_max=sv[:W, g, 8:16],
                                                                               in_values=sb2[:W, gi, :]), reads=[t2, tv], writes=[ti])
            join_trackers(scH[0], scG[0:8])
            join_trackers(scH[1], scG[8:16])
            join_trackers(sc2H[0], sc2G[0:8])
            join_trackers(sc2H[1], sc2G[8:16])
            join_trackers(sv, svG)
            join_trackers(siu, siuG)
            CP(sif[:W], siu[:W], [siu], [sif])
            svv = sv[:W].rearrange("p (h two) k -> p h two k", two=2)
            sfv = sif[:W].rearrange("p (h two) k -> p h two k", two=2)
            candH = [b_[:W].rearrange("p (h two) n -> p h (two n)", two=2) for b_ in scH]
            cand2H = [b_[:W].rearrange("p (h two) n -> p h (two n)", two=2) for b_ in sc2H]
            for q_ in range(2):
                TT(candH[q_].rearrange("p h (a b) -> p h a b", b=16),
                   svv[:, q_ * 4:(q_ + 1) * 4, 0, :].unsqueeze(3).broadcast_to([W, 4, 16, 16]),
                   svv[:, q_ * 4:(q_ + 1) * 4, 1, :].unsqueeze(2).broadcast_to([W, 4, 16, 16]), ALU.add, [sv], [scH[q_]])
            cG = sub_trackers(scH[0], 4, "h") + sub_trackers(scH[1], 4, "h")
            c2G = sub_trackers(sc2H[0], 4, "h") + sub_trackers(sc2H[1], 4, "h")
            csvG = sub_trackers(csv, 8, "h")
            ciuG = sub_trackers(ciu, 8, "h")
            for stage in range(5):
                for h in range(8):
                    cv, c2v = candH[h // 4][:, h % 4, :], cand2H[h // 4][:, h % 4, :]
                    t1, t2, tv, ti = cG[h], c2G[h], csvG[h], ciuG[h]
                    if stage == 0:
                        P.op("dve", lambda e, h=h, cv=cv: e.max(out=csv[:W, h, 0:8], in_=cv), reads=[t1], writes=[tv])
                    elif stage == 1:
                        P.op("dve", lambda e, h=h, cv=cv: e.max_index(out=ciu[:W, h, 0:8], in_max=csv[:W, h, 0:8], in_values=cv),
                             reads=[t1, tv], writes=[ti])
                    elif stage == 2:
                        P.op("dve", lambda e, h=h, cv=cv, c2v=c2v: e.match_replace(out=c2v, in_to_replace=csv[:W, h, 0:8],
                                                                                 in_values=cv, imm_value=NEG),
                             reads=[t1, tv], writes=[t2])
                    elif stage == 3:
                        P.op("dve", lambda e, h=h, c2v=c2v: e.max(out=csv[:W, h, 8:16], in_=c2v), reads=[t2], writes=[tv])
                    else:
                        P.op("dve", lambda e, h=h, c2v=c2v: e.max_index(out=ciu[:W, h, 8:16], in_max=csv[:W, h, 8:16], in_values=c2v),
                             reads=[t2, tv], writes=[ti])
            join_trackers(scH[0], cG[0:4])
            join_trackers(scH[1], cG[4:8])
            join_trackers(sc2H[0], c2G[0:4])
            join_trackers(sc2H[1], c2G[4:8])
            join_trackers(csv, csvG)
            join_trackers(ciu, ciuG)
            CP(cif[:W], ciu[:W], [ciu], [cif])
            for q_ in range(2):
                tb_ = t4H[q_]
                hs_ = slice(q_ * 4, q_ * 4 + 4)
                TT(tb_[:W, :, :, 0:15], cif[:W, hs_, :].unsqueeze(3).broadcast_to([W, 4, 16, 15]),
                   thr15[:W].unsqueeze(1).unsqueeze(1).broadcast_to([W, 4, 16, 15]), ALU.is_ge, [cif, cst], [tb_])
                RED(av[:W, hs_, :], tb_[:W, :, :, 0:15], [tb_], [av])
            STT(bv[:W], av[:W], -16.0, cif[:W], ALU.mult, ALU.add, [av, cif], [bv])
            for (sel, half, dst) in ((av, 0, iv), (bv, 1, jv)):
                for q_ in range(2):
                    tb_ = t4H[q_]
                    hs_ = slice(q_ * 4, q_ * 4 + 4)
                    TT(tb_[:W], sel[:W, hs_, :].unsqueeze(3).broadcast_to([W, 4, 16, 16]),
                       iota16[:W].unsqueeze(1).unsqueeze(1).broadcast_to([W, 4, 16, 16]), ALU.is_equal, [sel, cst], [tb_])
                    TT(tb_[:W], tb_[:W], sfv[:, hs_, half, :].unsqueeze(2).broadcast_to([W, 4, 16, 16]), ALU.mult, [tb_, sif], [tb_])
                    RED(dst[:W, hs_, :], tb_[:W], [tb_], [dst])
            STT(eidf[:W].rearrange("p (h k) -> p h k", k=16), iv[:W], 128.0, jv[:W], ALU.mult, ALU.add, [iv, jv], [eidf])
            if layer == 1:
                TS(eidf[:W], eidf[:W], 16384.0, None, ALU.add, None, [eidf], [eidf])
            CP(eidi[:W], eidf[:W], [eidf], [eidi])
            g3 = gate[:W].rearrange("p (h k) -> p h k", k=16)
            TT(g3, csv[:W], csv[:W, :, 0:1].broadcast_to([W, 8, 16]), ALU.subtract, [csv], [gate])
            ACT(gate[:W], gate[:W], AF.Exp, [gate], [gate])
            RED(gsum[:W], g3, [gate], [gsum])
            RECIP(gsum[:W], gsum[:W], [gsum], [gsum])
            TT(g3, g3, gsum[:W].unsqueeze(2).broadcast_to([W, 8, 16]), ALU.mult, [gate, gsum], [gate])
            NGS_ = int(os.environ.get('K_NGS', '12'))
            if NGS_ > 8:
                h2tbB = hT
                h2tb = flat(hT)[:W, 0:1024]
                prods = tuple((b_, flat(b_).bitcast(BF16)[:, 0:1024]) for b_ in (Fb[0], Fb[1], Fb[5]))
                dumpB = Fb[4]
                dumpv = flat(dumpB).bitcast(BF16)[:, 0:1024]
                extra = (scH[1], sc2H[0], sc2H[1], t4H[0], t4H[1], xtok)
            else:
                h2tbB = scH[1]
                h2tb = flat(h2tbB).bitcast(BF16)[:W, 0:1024]
                prods = tuple((b_, flat(b_).bitcast(BF16)[:, 0:1024]) for b_ in (sc2H[0], sc2H[1], t4H[0]))
                dumpB = t4H[1]
                dumpv = flat(dumpB).bitcast(BF16)[:, 0:1024]
                extra = ()
            CP(h2tb, h2tok[:W], [h2tok], [h2tbB], eng="pool")
            gs = []
            for i in range(NSLOT):
                gs.append((slots[i], flat(slots[i]).bitcast(BF16), f"g{i}"))
            for xb_ in (junk, h2tok, scH[0], acc) + extra:
                gs.append((xb_, flat(xb_).bitcast(BF16), "g_" + xb_.name))
            splits = []
            gs = gs[:NGS_]
            NG_ = len(gs)
            G = 4
            nrows = 128 if not os.environ.get('K_NOGATHER') else 0
            per_group = (len(P.pending) + 27) // 28 if P.pending else 0
            pacc = (PS[6], PS[7])
            for k0 in range(0, nrows, G):
                ksl = []
                for k in range(k0, k0 + G):
                    sl, slv, skey = gs[gslot[0] % NG_]
                    gslot[0] += 1
                    ksl.append((sl, slv))
                    P.op("pool", lambda e, slv=slv, k=k: e.indirect_dma_start(
                        out=slv[:W, :], out_offset=None, in_=uvb,
                        in_offset=bass.IndirectOffsetOnAxis(ap=eidi[:W, k:k + 1], axis=0)),
                        reads=[eidi] + uvsL, writes=[sl], dma=skey)
                    pbuf, pv = prods[k % 3]
                    if os.environ.get('K_STT'):
                        STT(pv[:W, :], slv[:W, 0:1024], 1.0, h2tb, ALU.mult, ALU.mult, [sl, h2tbB], [pbuf])
                    else:
                        TT(pv[:W, :], slv[:W, 0:1024], h2tb, ALU.mult, [sl, h2tbB], [pbuf])
                    ACT(dumpv[:W, :], pv[:W, :], AF.Copy, [pbuf], [dumpB, hid], accum=hid[:W, k:k + 1])
                ACT(actv[:W, k0:k0 + G], hid[:W, k0:k0 + G], AF.Gelu, [hid], [actv])
                TT(actv[:W, k0:k0 + G], actv[:W, k0:k0 + G], gate[:W, k0:k0 + G], ALU.mult, [actv, gate], [actv])
                for k in range(k0, k0 + G):
                    sl, slv = ksl[k - k0]
                    dk = DkB[k % 4]
                    TS(dk[:W, :W], identb[:W, :W], actv[:W, k:k + 1], None, ALU.mult, None, [cstb, actv], [dk])
                    MM(pacc[0][:W, :512], dk[:W, :W], slv[:W, 1024:1536], k == 0, k == 127, [dk, sl], pacc[0])
                    MM(pacc[1][:W, :512], dk[:W, :W], slv[:W, 1536:2048], k == 0, k == 127, [dk, sl], pacc[1])
                P.replay_pending(per_group)
            P.replay_pending()
            if nrows:
                CP(acc[:W, 0:512], pacc[0][:W, :512], [pacc[0]], [acc], eng="act")
                CP(acc[:W, 512:1024], pacc[1][:W, :512], [pacc[1]], [acc])
            for parent, subs in splits:
                for sbuf_ in subs:
                    for kk_, vv_ in list(sbuf_.rd.items()) + ([sbuf_.lw] if sbuf_.lw else []):
                        if parent.rd.get(kk_, 0) < vv_:
                            parent.rd[kk_] = vv_
            if final_out is None:
                for half in range(2):
                    pb = bank()
                    for c4 in range(4):
                        c = half * 4 + c4
                        TR(pb[:, c4 * W:(c4 + 1) * W], acc[:W, c * 128:(c + 1) * 128], ident[:W, :W], [acc, cst], pb)
                    TT(XT_()[:, half * 4:(half + 1) * 4, :W], XT_()[:, half * 4:(half + 1) * 4, :W],
                       pb[:, :4 * W].rearrange("p (c w) -> p c w", w=W), ALU.add, [XT_(), pb], [XT_()])
            else:
                final_norm(W, final_out, add_acc=True)

        def final_norm(W, out_ap, add_acc):
            for half in range(2):
                pb = bank()
                for c4 in range(4):
                    c = half * 4 + c4
                    TR(pb[:W, c4 * 128:(c4 + 1) * 128], XT_()[:, c, :W], ident, [XT_(), cst], pb)
                if add_acc:
                    TT(acc[:W, half * 512:(half + 1) * 512], acc[:W, half * 512:(half + 1) * 512], pb[:W, :512], ALU.add,
                       [acc, pb], [acc])
                else:
                    CP(acc[:W, half * 512:(half + 1) * 512], pb[:W, :512], [pb], [acc])
            STT(junk[:W], acc[:W], 1.0, acc[:W], ALU.mult, ALU.mult, [acc], [junk, fssq], accum=fssq[:W, 0:1])
            ACT(fssq[:W, 1:2], fssq[:W, 0:1], AF.Sqrt, [fssq, eps_col], [fssq], bias=eps_col[:W, 0:1], scale=1.0 / D)
            RECIP(fssq[:W, 1:2], fssq[:W, 1:2], [fssq], [fssq])
            STT(xtok[:W], acc[:W], fssq[:W, 1:2], fingb[:W], ALU.mult, ALU.mult, [acc, fssq, fingb], [xtok])
            DMA(out_ap, xtok[:W, :], [xtok], [], "st_xtok")

        def zero_states():
            P.op("pool", lambda e: e.memset(Sa[:], 0.0), writes=[Sa])
            P.op("pool", lambda e: e.memset(Sg[:], 0.0), writes=[Sg])
            P.op("pool", lambda e: e.memset(SabC[0][:], 0.0), writes=[SabC[0]])
            P.op("pool", lambda e: e.memset(SgbC[0][:], 0.0), writes=[SgbC[0]])

        def zero_l1():
            P.op("pool", lambda e: e.memset(Pst[:], 0.0), writes=[Pst])
            P.op("pool", lambda e: e.memset(HS[:, :, 0:1], 0.0), writes=[HS])

        def load_l1(b):
            DMA(junk[0:64, :].rearrange("p (h k) -> p h k", k=64), st_r[b].rearrange("h i k -> i h k"), [], [junk], "junk")
            pb = bank()
            for c in range(8):
                MM(pb[:, c * 64:(c + 1) * 64], junk[0:64, c * 128:(c + 1) * 128], ident[0:64, 0:64], True, True, [junk, cst], pb)
            CP(flat(Pst), pb[:, :512], [pb], [Pst])
            DMA(acc[0:8, 0:128], st_s[b].rearrange("(c p) -> c p", p=128), [], [acc], "acc")
            pb = bank()
            MM(pb[:, 0:8], acc[0:8, 0:128], ident[0:8, 0:8], True, True, [acc, cst], pb)
            CP(HS[:, :, 0], pb[:, 0:8], [pb], [HS])

        def store_l1(orr, oss, b):
            for half in range(2):
                pb = bank()
                for u in range(4):
                    c = half * 4 + u
                    MM(pb[0:64, u * 128:(u + 1) * 128], Pst[:, c, :], ident, True, True, [Pst, cst], pb)
                CP(acc[0:64, half * 512:(half + 1) * 512], pb[0:64, :512], [pb], [acc])
            DMA(orr[b].rearrange("h i k -> i h k"), acc[0:64, :].rearrange("p (h k) -> p h k", k=64), [acc], [], "st_acc")
            pb = bank()
            MM(pb[0:8, 0:128], HS[:, :, 0], ident, True, True, [HS, cst], pb)
            CP(h2tok[0:8, 0:128], pb[0:8, 0:128], [pb], [h2tok])
            DMA(oss[b].rearrange("(c p) -> c p", p=128), h2tok[0:8, 0:128], [h2tok], [], "st_h2tok")

        def load_states(b):
            DMA(Sa[:], st_h[b].rearrange("h k v -> k h v"), [], [Sa], "Sa")
            DMA(Sg[:], st_g[b].rearrange("(j hh) k v -> (hh k) j v", hh=2), [], [Sg], "Sg")
            CP(SabC[0][:], Sa[:], [Sa], [SabC[0]])
            CP(SgbC[0][:], Sg[:], [Sg], [SgbC[0]])

        def store_states(oh, og, b):
            DMA(oh[b].rearrange("h k v -> k h v"), Sa[:], [Sa], [], "st_Sa")
            DMA(og[b].rearrange("(j hh) k v -> (hh k) j v", hh=2), Sg[:], [Sg], [], "st_Sg")

        PREF = int(os.environ.get('K_PREF', '0'))
        PMODE = int(os.environ.get('K_PMODE', '1'))

        def front(src_ap, W, cl):
            load_x(src_ap, W)
            if dbg >= 2:
                l0_mixer(W, cl)

        def back(out_ap, W, cl, nxt=None):
            if do_l0peer:
                peer(0, W)
            if do_l1:
                l1_mixer(W, cl)
            if nxt is not None:
                cur = xcur[0]
                xcur[0] = 1 - cur
                P.capture = []
                if PMODE == 1:
                    load_x(nxt[0], nxt[1])
                else:
                    front(*nxt)
                P.pending = P.capture
                P.capture = None
                xcur[0] = cur
            if do_l1peer:
                peer(1, W, final_out=out_ap)
            else:
                final_norm(W, out_ap, add_acc=False)
            P.replay_pending()
            if nxt is not None:
                xcur[0] = 1 - xcur[0]
                if PMODE == 1 and dbg >= 2:
                    l0_mixer(nxt[1], nxt[2])

        for b in range(2 if dbg >= 1 else 0):
            zero_states()
            zero_l1()
            front(xp[b, 0:128, :], 128, 64)
            for j in range(NSP):
                nxt = None
                if j + 1 < NSP:
                    nxt = (xp[b, (j + 1) * 128:(j + 2) * 128, :], 128, 64)
                if PREF and nxt is not None and do_l1peer:
                    back(yp[b, j * 128:(j + 1) * 128, :], 128, 64, nxt)
                else:
                    back(yp[b, j * 128:(j + 1) * 128, :], 128, 64, None)
                    if nxt is not None:
                        front(*nxt)
            store_states(o_ph, o_pg, b)
            if do_l1:
                store_l1(o_pr, o_ps, b)
        for b in range(2 if dbg >= 1 else 0):
            load_states(b)
            load_l1(b)
            front(xs[b], 32, 32)
            back(ys[b], 32, 32, None)
            store_states(o_sh, o_sg, b)
            if do_l1:
                store_l1(o_sr, o_ss, b)

        P.final_wait("sp")
        P.build(es)
    return nc, P.nins


def make_consts():
    c = np.zeros((128, 1024), np.float32)
    c[:, 0:128] = np.eye(128, dtype=np.float32)
    s = np.arange(128)[:, None]
    t = np.arange(128)[None, :]
    c[:, 128:256] = ((s <= t) & ((s // 64) == (t // 64))).astype(np.float32)
    c[:, 256:384] = 1.0
    c[:, 384:512] = ((s // 64) == (t // 64)).astype(np.float32)
    c[:, 512:640] = (np.arange(128) % 64 != 0).astype(np.float32)[None, :]
    c[:, 640:656] = np.arange(16, dtype=np.float32)[None, :]
    c[:, 656:671] = (16.0 * np.arange(1, 16, dtype=np.float32))[None, :]
    c[:, 672:800] = (np.arange(128) % 32 != 0).astype(np.float32)[None, :]
    return c


def make_consts2():
    c = np.zeros((128, NC2), np.float32)
    off = 0
    for cl in (64, 32):
        n = 2 * cl
        r = np.arange(n)[:, None]
        q = np.arange(n)[None, :]
        same = (r // cl) == (q // cl)
        lt = (r % cl) < (q % cl)
        t = np.arange(cl)[None, :]
        le = (r % cl) <= t
        m1 = np.concatenate([-(same & lt).astype(np.float32), le.astype(np.float32)], axis=1)
        m2 = np.concatenate([(same & lt).astype(np.float32), le.astype(np.float32)], axis=1)
        m3 = -(same & ((r % cl) > (q % cl))).astype(np.float32)
        bmt = ((r // cl) == (np.arange(128)[None, :] // 64)).astype(np.float32)
        for m in (m1, m2, m3, bmt):
            c[:n, off:off + m.shape[1]] = m
            off += m.shape[1]
    assert off == 1024
    c[:, 1024:1088] = (np.arange(128)[:, None] % 64 == np.arange(64)[None, :]).astype(np.float32)
    return c


def make_pcols(inp):
    pcm = np.zeros((128, NPCOL), np.float32)

    def put(name, vec):
        v = np.asarray(vec, np.float32).reshape(-1, 128).T
        pcm[:, PCOL[name]:PCOL[name] + v.shape[1]] = v
    put("n1g0", inp["norm1_g"][0])
    put("n1g1", inp["norm1_g"][1])
    put("n2g0", inp["norm2_g"][0])
    put("n2g1", inp["norm2_g"][1])
    put("lbraw", np.asarray(inp["hgrn_lower_bounds"]).reshape(-1))
    put("hng", inp["hgrn_norm_g"][0])
    put("gng", inp["gla_norm_g"][0])
    put("ggb", inp["gla_gate_b"][0])
    put("mu", np.asarray(inp["rwkv_mu"][0]).reshape(-1))
    put("w0", inp["rwkv_w0"][0])
    put("a0", inp["rwkv_a0"][0])
    put("kk", inp["rwkv_k_k"][0])
    put("ka", inp["rwkv_k_a"][0])
    put("rk", np.asarray(inp["rwkv_r_k"][0]).reshape(-1))
    put("lng", inp["rwkv_ln_g"][0])
    put("lnb", inp["rwkv_ln_b"][0])
    return pcm


_CACHE = {}


def make_in_maps(inp, ncores=NCORES):
    consts = make_consts()
    pcm = make_pcols(inp)
    skT = np.ascontiguousarray(np.transpose(inp["peer_sub_keys"].reshape(2, 16, 128, 128), (0, 3, 1, 2)))
    shared = {
        "pcols": pcm, "consts": consts, "fing": np.ascontiguousarray(inp["final_g"].reshape(1, D)),
        "w_in": np.ascontiguousarray(inp["w_in_ab"][0]), "ggw2": np.ascontiguousarray(inp["gla_gate_w2"][0]),
        "w_oab": np.ascontiguousarray(inp["w_out_ab"][0]), "w_q": np.ascontiguousarray(inp["peer_w_q"]),
        "w_rkv": np.ascontiguousarray(inp["rwkv_w_rkv"][0]), "w_oc": np.ascontiguousarray(inp["w_out_c"][0]),
        "w_w1": np.ascontiguousarray(inp["rwkv_w_w1"][0]), "a_w1": np.ascontiguousarray(inp["rwkv_a_w1"][0]),
        "g_w1": np.ascontiguousarray(inp["rwkv_g_w1"][0]), "w_w2": np.ascontiguousarray(inp["rwkv_w_w2"][0]),
        "a_w2": np.ascontiguousarray(inp["rwkv_a_w2"][0]), "g_w2": np.ascontiguousarray(inp["rwkv_g_w2"][0]),
        "consts2": make_consts2(),
        "skT": skT, "pu": np.ascontiguousarray(inp["peer_u"]), "pv": np.ascontiguousarray(inp["peer_v"]),
    }
    in_maps = []
    for c in range(ncores):
        sl = slice(2 * c, 2 * c + 2)
        m = dict(shared)
        m["xp"] = np.ascontiguousarray(inp["x_prompt"][sl])
        m["xs"] = np.ascontiguousarray(inp["x_sample"][sl])
        m["st_h"] = np.ascontiguousarray(inp["state_hgrn"][0, sl])
        m["st_g"] = np.ascontiguousarray(inp["state_gla"][0, sl])
        m["st_r"] = np.ascontiguousarray(inp["state_rwkv"][0, sl])
        m["st_s"] = np.ascontiguousarray(inp["state_shift"][0, sl])
        in_maps.append(m)
    return in_maps


def kernel(**inp):
    inp = {k: np.asarray(v) for k, v in inp.items()}
    SEQ = inp["x_prompt"].shape[1]
    key = SEQ
    if key not in _CACHE:
        _CACHE[key] = build_program(SEQ)
    nc, _ = _CACHE[key]
    in_maps = make_in_maps(inp)
    res = run_bass_kernel_spmd(nc, in_maps, core_ids=list(range(NCORES)))
    R = res.results

    def cat(name):
        return np.concatenate([np.asarray(r[name]) for r in R], axis=0)
    y_p = cat("yp")
    y_s = cat("ys")
    return (y_p, y_s, cat("o_ph")[None], cat("o_pg")[None], cat("o_pr")[None], cat("o_ps")[None],
            cat("o_sh")[None], cat("o_sg")[None], cat("o_sr")[None], cat("o_ss")[None])
```
